# Optimizing a Trainium2 kernel written in Bass

```python
import math
import jax, jax.numpy as jnp
from jax import lax
import numpy as np

D_MODEL = 4096
BATCH = 4
SEQ = 4096
DEPTH = 1

D_MIX = D_MODEL
D_SSM = D_MIX // 2
D_ATTN = D_MIX - D_SSM
SSM_HEAD_DIM = 64
SSM_HEADS = D_SSM // SSM_HEAD_DIM
SSM_GROUPS = 4
SSM_HPG = SSM_HEADS // SSM_GROUPS
SSM_STATE = 128
CONV_WIDTH = 4
CONV_CH = D_SSM + 2 * SSM_GROUPS * SSM_STATE
CHUNK = 128
ATTN_HEAD_DIM = 64
ATTN_VDIM = 2 * ATTN_HEAD_DIM
ATTN_HEADS = D_ATTN // ATTN_VDIM
Q_BLOCK = 128
NORM_EPS = 1e-6
IN_COLS = D_MIX + CONV_CH + SSM_HEADS + 3 * D_ATTN

kernel_name = "hybrid_ssd_diffattn_parallel_heads"


def rms_norm(x, w, eps=NORM_EPS):
    xf = x.astype(jnp.float32)
    y = xf * lax.rsqrt(jnp.mean(xf * xf, axis=-1, keepdims=True) + eps)
    return (y * w.astype(jnp.float32)).astype(x.dtype)


def segsum_exp(a):
    t = a.shape[-1]
    cs = jnp.cumsum(a, axis=-1)
    diff = cs[..., :, None] - cs[..., None, :]
    mask = jnp.tril(jnp.ones((t, t), dtype=bool))
    return jnp.exp(jnp.where(mask, diff, -jnp.inf))


def causal_depthwise_conv(u, w, b):
    y = lax.conv_general_dilated(
        u, w[:, None, :].astype(u.dtype), window_strides=(1,),
        padding=[(CONV_WIDTH - 1, 0)], dimension_numbers=("NWC", "WIO", "NWC"),
        feature_group_count=u.shape[-1])
    return y + b.astype(u.dtype)


def ssd_chunked(xh, dt, a_head, bg, cg):
    b, s = xh.shape[:2]
    nc = s // CHUNK
    X = (xh * dt[..., None]).reshape(b, nc, CHUNK, SSM_GROUPS, SSM_HPG, SSM_HEAD_DIM)
    a = jnp.moveaxis((dt * a_head).reshape(b, nc, CHUNK, SSM_GROUPS, SSM_HPG), 2, -1)
    Bc = bg.reshape(b, nc, CHUNK, SSM_GROUPS, SSM_STATE)
    Cc = cg.reshape(b, nc, CHUNK, SSM_GROUPS, SSM_STATE)
    a_cs = jnp.cumsum(a, axis=-1)
    L = segsum_exp(a)
    CB = jnp.einsum("bclgn,bcsgn->bcgls", Cc, Bc)
    y_diag = jnp.einsum("bcgls,bcgrls,bcsgrp->bclgrp", CB, L, X)
    decay_states = jnp.exp(a_cs[..., -1:] - a_cs)
    states = jnp.einsum("bclgn,bcgrl,bclgrp->bcgrpn", Bc, decay_states, X)
    states = jnp.concatenate([jnp.zeros_like(states[:, :1]), states], axis=1)
    chunk_a = jnp.moveaxis(a_cs[..., -1], 1, -1)
    chunk_a = jnp.pad(chunk_a, ((0, 0), (0, 0), (0, 0), (1, 0)))
    decay_chunk = segsum_exp(chunk_a)
    states_in = jnp.einsum("bgrzc,bcgrpn->bzgrpn", decay_chunk, states)[:, :-1]
    y_off = jnp.einsum("bclgn,bcgrpn,bcgrl->bclgrp", Cc, states_in, jnp.exp(a_cs))
    return (y_diag + y_off).reshape(b, s, SSM_GROUPS, SSM_HPG, SSM_HEAD_DIM)


def diff_attention(q, k, v, lam):
    s = q.shape[1]
    scale = ATTN_HEAD_DIM ** -0.5
    outs = []
    for i in range(s // Q_BLOCK):
        q0 = i * Q_BLOCK
        kv = q0 + Q_BLOCK
        qb = q[:, q0:kv]
        scores = jnp.einsum("bqhmd,bkhmd->bhmqk", qb, k[:, :kv]).astype(jnp.float32) * scale
        mask = jnp.arange(kv)[None, :] <= (q0 + jnp.arange(Q_BLOCK))[:, None]
        probs = jax.nn.softmax(jnp.where(mask, scores, -jnp.inf), axis=-1)
        attn = probs[:, :, 0] - lam * probs[:, :, 1]
        outs.append(jnp.einsum("bhqk,bkhe->bqhe", attn.astype(v.dtype), v[:, :kv]))
    return jnp.concatenate(outs, axis=1)


def hybrid_layer(x, layer, pre_norm_w, post_norm_w, w_in, conv_w, conv_b, dt_bias, a_log,
                 d_skip, ssm_norm_w, lambda_q1, lambda_k1, lambda_q2, lambda_k2,
                 attn_subln_w, w_out):
    b, s, _ = x.shape
    h = rms_norm(x, pre_norm_w)
    proj = h @ w_in.astype(h.dtype)
    i0 = D_MIX
    i1 = i0 + CONV_CH
    i2 = i1 + SSM_HEADS
    i3 = i2 + D_ATTN
    i4 = i3 + D_ATTN
    z, xbc, dt_raw, q, k, v = jnp.split(proj, [i0, i1, i2, i3, i4], axis=-1)
    z_ssm, z_attn = jnp.split(z, [D_SSM], axis=-1)

    xbc = jax.nn.silu(causal_depthwise_conv(xbc, conv_w, conv_b)).astype(jnp.float32)
    xs, bs, cs = jnp.split(xbc, [D_SSM, D_SSM + SSM_GROUPS * SSM_STATE], axis=-1)
    dt = jax.nn.softplus(dt_raw.astype(jnp.float32) + dt_bias.astype(jnp.float32))
    a_head = -jnp.exp(a_log.astype(jnp.float32)).reshape(SSM_GROUPS, SSM_HPG)
    xh = xs.reshape(b, s, SSM_GROUPS, SSM_HPG, SSM_HEAD_DIM)
    y = ssd_chunked(xh, dt.reshape(b, s, SSM_GROUPS, SSM_HPG), a_head,
                    bs.reshape(b, s, SSM_GROUPS, SSM_STATE), cs.reshape(b, s, SSM_GROUPS, SSM_STATE))
    y = y + d_skip.astype(jnp.float32).reshape(SSM_GROUPS, SSM_HPG)[..., None] * xh
    y = y.reshape(b, s, D_SSM).astype(x.dtype) * jax.nn.silu(z_ssm)
    gs = D_SSM // SSM_GROUPS
    y_ssm = rms_norm(y.reshape(b, s, SSM_GROUPS, gs), ssm_norm_w.reshape(SSM_GROUPS, gs))
    y_ssm = y_ssm.reshape(b, s, D_SSM)

    lam_init = 0.8 - 0.6 * math.exp(-0.3 * layer)
    lam = (jnp.exp(jnp.sum(lambda_q1.astype(jnp.float32) * lambda_k1.astype(jnp.float32)))
           - jnp.exp(jnp.sum(lambda_q2.astype(jnp.float32) * lambda_k2.astype(jnp.float32)))
           + lam_init)
    qh = q.reshape(b, s, ATTN_HEADS, 2, ATTN_HEAD_DIM)
    kh = k.reshape(b, s, ATTN_HEADS, 2, ATTN_HEAD_DIM)
    vh = v.reshape(b, s, ATTN_HEADS, ATTN_VDIM)
    o = diff_attention(qh, kh, vh, lam)
    o = rms_norm(o, attn_subln_w) * (1.0 - lam_init)
    y_attn = o.reshape(b, s, D_ATTN) * jax.nn.silu(z_attn)

    out = jnp.concatenate([y_ssm, y_attn], axis=-1) @ w_out.astype(x.dtype)
    return x + rms_norm(out, post_norm_w)


def setup_inputs(seed: int = 0) -> dict:
    key = jax.random.key(seed)
    ks = jax.random.split(key, 20)
    f32 = jnp.float32
    x = jax.random.normal(ks[0], (BATCH, SEQ, D_MODEL), f32)
    pre_norm_w = 1.0 + 0.02 * jax.random.normal(ks[1], (DEPTH, D_MODEL), f32)
    post_norm_w = 1.0 + 0.02 * jax.random.normal(ks[2], (DEPTH, D_MODEL), f32)
    w_in = jax.random.normal(ks[3], (DEPTH, D_MODEL, IN_COLS), f32) * D_MODEL ** -0.5
    conv_w = jax.random.normal(ks[4], (DEPTH, CONV_WIDTH, CONV_CH), f32) * CONV_WIDTH ** -0.5
    conv_b = 0.02 * jax.random.normal(ks[5], (DEPTH, CONV_CH), f32)
    dt0 = jnp.exp(jax.random.uniform(ks[6], (DEPTH, SSM_HEADS), f32,
                                     math.log(1e-3), math.log(1e-1)))
    dt_bias = dt0 + jnp.log(-jnp.expm1(-dt0))
    a_log = jnp.log(jax.random.uniform(ks[7], (DEPTH, SSM_HEADS), f32, 1.0, 16.0))
    d_skip = 1.0 + 0.1 * jax.random.normal(ks[8], (DEPTH, SSM_HEADS), f32)
    ssm_norm_w = 1.0 + 0.02 * jax.random.normal(ks[9], (DEPTH, D_SSM), f32)
    lambda_q1 = 0.1 * jax.random.normal(ks[10], (DEPTH, ATTN_HEAD_DIM), f32)
    lambda_k1 = 0.1 * jax.random.normal(ks[11], (DEPTH, ATTN_HEAD_DIM), f32)
    lambda_q2 = 0.1 * jax.random.normal(ks[12], (DEPTH, ATTN_HEAD_DIM), f32)
    lambda_k2 = 0.1 * jax.random.normal(ks[13], (DEPTH, ATTN_HEAD_DIM), f32)
    attn_subln_w = 1.0 + 0.02 * jax.random.normal(ks[14], (DEPTH, ATTN_VDIM), f32)
    w_out = jax.random.normal(ks[15], (DEPTH, D_MIX, D_MODEL), f32) * D_MIX ** -0.5
    return {"x": x, "pre_norm_w": pre_norm_w, "post_norm_w": post_norm_w, "w_in": w_in,
            "conv_w": conv_w, "conv_b": conv_b, "dt_bias": dt_bias, "a_log": a_log,
            "d_skip": d_skip, "ssm_norm_w": ssm_norm_w, "lambda_q1": lambda_q1,
            "lambda_k1": lambda_k1, "lambda_q2": lambda_q2, "lambda_k2": lambda_k2,
            "attn_subln_w": attn_subln_w, "w_out": w_out}


def reference(x, pre_norm_w, post_norm_w, w_in, conv_w, conv_b, dt_bias, a_log, d_skip,
              ssm_norm_w, lambda_q1, lambda_k1, lambda_q2, lambda_k2, attn_subln_w, w_out):
    for layer in range(DEPTH):
        x = hybrid_layer(x, layer, pre_norm_w[layer], post_norm_w[layer], w_in[layer],
                         conv_w[layer], conv_b[layer], dt_bias[layer], a_log[layer],
                         d_skip[layer], ssm_norm_w[layer], lambda_q1[layer], lambda_k1[layer],
                         lambda_q2[layer], lambda_k2[layer], attn_subln_w[layer], w_out[layer])
    return x
```

```python
import types
import numpy as np
import ml_dtypes
import concourse.bass as bass
import concourse.mybir as mybir
from concourse.bass_utils import run_bass_kernel_spmd

F32 = mybir.dt.float32
BF16 = mybir.dt.bfloat16
AF = mybir.ActivationFunctionType
ALU = mybir.AluOpType
AX = mybir.AxisListType

ENGS = ("pe", "act", "dve", "pool", "sp")
STRICT_SAME_ENGINE = True


def _freeze(fn):
    if fn is None or fn.__closure__ is None:
        return fn
    cells = []
    for c in fn.__closure__:
        try:
            cells.append(types.CellType(c.cell_contents))
        except ValueError:
            cells.append(c)
    return types.FunctionType(fn.__code__, fn.__globals__, fn.__name__, fn.__defaults__, tuple(cells))


class Op:
    __slots__ = ("eng", "fn", "deps", "dma", "signal", "count", "src", "name", "inc")


class Prog:
    def __init__(self, nc):
        self.nc = nc
        self.ops = {e: [] for e in ENGS}
        self.last_w = {}
        self.readers = {}
        self.dma_cnt = {}
        self.out_dmas = []
        self.last_op = {}
        self.frozen = False
        self.check = False
        self.stop = None

    def add(self, eng, fn, reads=(), writes=(), dma=None, ndma=1, name=None, out=False, inc=16):
        if self.frozen and not out:
            return None
        op = Op()
        op.eng, op.fn, op.dma, op.signal, op.count, op.name = eng, _freeze(fn), dma, False, None, name
        op.deps = {}
        op.inc = inc
        is_async = dma is not None

        def dep(d, raw):
            if d is None or d is op:
                return
            if d.dma is None and not is_async and d.eng == eng:
                if eng == "pe" or not (raw or STRICT_SAME_ENGINE):
                    return
            cur = op.deps.get(d.src)
            if cur is None or cur.count_key() < d.count_key():
                op.deps[d.src] = d

        for k in reads:
            dep(self.last_w.get(k), True)
        for k in writes:
            dep(self.last_w.get(k), False)
            for r in self.readers.get(k, {}).values():
                dep(r, False)
        if is_async:
            c = self.dma_cnt.get(dma, 0) + inc * ndma
            self.dma_cnt[dma] = c
            op.src = ("dma", dma)
            op.count = c
        else:
            op.src = eng
            op.count = len(self.ops[eng])
        for k in writes:
            self.last_w[k] = op
            self.readers[k] = {}
        for k in reads:
            self.readers.setdefault(k, {})[op.src] = op
        self.ops[eng].append(op)
        self.last_op[op.src] = op
        if out:
            self.out_dmas.append(op)
        return op

    def mark(self, name):
        if self.stop is not None and name == self.stop:
            self.frozen = True

    def barrier(self):
        lasts = dict(self.last_op)
        for e in ENGS:
            op = Op()
            op.eng, op.fn, op.dma, op.signal, op.count, op.name = e, None, None, False, None, "bar"
            op.inc = 16
            op.deps = {s: d for s, d in lasts.items() if s != e}
            op.src = e
            op.count = len(self.ops[e])
            self.ops[e].append(op)
        self.last_w = {}
        self.readers = {}

    def simulate(self):
        sem = {}
        pc = {e: 0 for e in ENGS}
        progress = True
        while progress:
            progress = False
            for e in ENGS:
                while pc[e] < len(self.ops[e]):
                    op = self.ops[e][pc[e]]
                    if any(sem.get(src, 0) < d.count for src, d in op.deps.items()):
                        break
                    if op.fn is not None:
                        if op.dma is not None:
                            sem[op.src] = sem.get(op.src, 0) + op.inc
                        elif op.signal:
                            sem[e] = sem.get(e, 0) + 1
                            assert sem[e] == op.count, (e, sem[e], op.count)
                    pc[e] += 1
                    progress = True
        stuck = {e: (pc[e], len(self.ops[e])) for e in ENGS if pc[e] < len(self.ops[e])}
        for e in stuck:
            op = self.ops[e][pc[e]]
            print("STUCK", e, pc[e], op.name, {src: (d.count, sem.get(src, 0)) for src, d in op.deps.items()})
        assert not stuck, stuck
        nw = {e: 0 for e in ENGS}
        for e in ENGS:
            seen = {}
            for op in self.ops[e]:
                for src, d in op.deps.items():
                    if seen.get(src, 0) < d.count:
                        seen[src] = d.count
                        nw[e] += 1
        print("simulate OK", {e: len(self.ops[e]) for e in ENGS}, "waits", nw)

    def emit(self):
        nc = self.nc
        fin = Op()
        fin.eng, fin.fn, fin.dma, fin.signal, fin.count, fin.name = "sp", None, None, False, None, "fin"
        fin.deps = {}
        for d in self.out_dmas:
            cur = fin.deps.get(d.src)
            if cur is None or cur.count < d.count:
                fin.deps[d.src] = d
        fin.src = "sp"
        self.ops["sp"].append(fin)
        for e in ENGS:
            for op in self.ops[e]:
                for d in op.deps.values():
                    d.signal = True
        for e in ENGS:
            c = 0
            for op in self.ops[e]:
                if op.dma is None:
                    if op.signal:
                        c += 1
                        op.count = c
                    else:
                        op.count = None
        srcs = list(ENGS) + [("dma", k) for k in self.dma_cnt]
        if self.check:
            self.simulate()
        from contextlib import ExitStack
        with ExitStack() as st:
            sems = {}
            for i, s in enumerate(srcs):
                sems[s] = st.enter_context(nc.semaphore(f"s{i}"))
            block = st.enter_context(nc.Block())
            engobj = {"pe": nc.tensor, "act": nc.scalar, "dve": nc.vector,
                      "pool": nc.gpsimd, "sp": nc.sync}

            def stream(e):
                def body(eng):
                    seen = {}
                    for op in self.ops[e]:
                        for src, d in op.deps.items():
                            if seen.get(src, 0) >= d.count:
                                continue
                            eng.wait_ge(sems[src], d.count)
                            seen[src] = d.count
                        if op.fn is None:
                            continue
                        r = op.fn(eng)
                        if op.dma is not None:
                            rl = r if isinstance(r, (list, tuple)) else [r]
                            for ins in rl:
                                ins.then_inc(sems[op.src], op.inc)
                        elif op.signal:
                            ins = r[-1] if isinstance(r, (list, tuple)) else r
                            ins.then_inc(sems[e], 1)
                return body

            block.tensor(stream("pe"))
            block.scalar(stream("act"))
            block.vector(stream("dve"))
            block.gpsimd(stream("pool"))
            block.sync(stream("sp"))


def _ck(self):
    return self.count


Op.count_key = _ck


D = 4096
SEQ = 4096
NBATCH = 4
OWN = 2048
TT = 512
NTILE = SEQ // TT
KC = D // 128
NG = 4
NH = 16
EPS = 1e-6
LAM_INIT = 0.8 - 0.6 * 1.0
NEG = -30000.0

PANELS = []
for _g in range(NG):
    PANELS.append((f"xs{_g}", _g * 1280, 512))
    PANELS.append((f"bc{_g}", _g * 1280 + 512, 256))
    PANELS.append((f"z{_g}", _g * 1280 + 768, 512))
for _i in range(4):
    PANELS.append((f"k{_i}", 5120 + _i * 512, 512))
for _i in range(4):
    PANELS.append((f"v{_i}", 7168 + _i * 512, 512))
for _i in range(4):
    PANELS.append((f"q{_i}", 9216 + _i * 512, 512))
for _i in range(4):
    PANELS.append((f"za{_i}", 11264 + _i * 512, 512))
NCOL = 13312


def _perm_cols():
    idx = []
    for g in range(NG):
        idx += list(range(4096 + g * 512, 4096 + (g + 1) * 512))
        idx += list(range(4096 + 2048 + g * 128, 4096 + 2048 + (g + 1) * 128))
        idx += list(range(4096 + 2560 + g * 128, 4096 + 2560 + (g + 1) * 128))
        idx += list(range(g * 512, (g + 1) * 512))
    idx += list(range(9248, 11296))
    idx += list(range(11296, 13344))
    idx += list(range(7200, 9248))
    idx += list(range(2048, 4096))
    return np.array(idx, dtype=np.int64)


def build_program(ntiles=NTILE, do_attn=True, do_out=True, dump=False, ncast=3, stop=None):
    nc = bass.Bass("TRN2", target_bir_lowering=False)
    P = Prog(nc)
    P.stop = stop
    A = P.add

    def din(name, shape, dt=F32):
        return nc.dram_tensor(name, list(shape), dt, kind="ExternalInput").ap()

    def dscr(name, shape, dt=BF16):
        return nc.dram_tensor(name, list(shape), dt).ap()

    x_d = din("x", [SEQ, D])
    win_d = din("w_in", [D, NCOL])
    wdt_d = din("w_dt", [D, 32])
    wout_d = din("w_out", [D, D])
    prew_d = din("pre_w", [128, KC])
    postw_d = din("post_w", [128, D])
    convw_d = din("conv_w", [128, 24, 4])
    convb_d = din("conv_b", [128, 24])
    hv_d = din("headvec", [128, 3, 32])
    snw_d = din("ssm_norm_w", [128, 2048])
    lam_d = din("lam", [128, 4, 64])
    subw_d = din("subln_w", [128, 128])
    pmask_d = din("pmask", [128, 1])
    pbias_d = din("pbias", [128, 1])
    subc_d = din("subw_col", [128, 1])
    trif_d = din("tri_f", [128, 128])
    negm_d = din("negm", [128, 128])
    trib_d = din("tri_b", [128, 128], BF16)
    mask2_d = din("mask2", [128, 2, 256], BF16)
    identb_d = din("ident_b", [128, 128], BF16)
    onesf_d = din("ones_f", [128, 128])
    out_d = nc.dram_tensor("out", [OWN, D], F32, kind="ExternalOutput").ap()

    wbp = {n: dscr("wb_" + n, [128, KC, w]) for (n, _, w) in PANELS}
    wbdt = dscr("wb_dt", [128, KC, 32])
    wbo = [dscr(f"wb_o{i}", [128, KC, 512]) for i in range(8)]
    kT_d = dscr("kT_d", [NH, 128, SEQ])
    qT_d = dscr("qT_d", [NH, 128, OWN])
    v_d = dscr("v_d", [SEQ, NH * 128])
    zaT_d = dscr("zaT_d", [NH, 128, OWN])
    yT_d = dscr("yT_d", [D, OWN])

    dbg_out = {}

    def ddump(name, shape, dt=F32):
        t = nc.dram_tensor("dbg_" + name, list(shape), dt, kind="ExternalOutput").ap()
        dbg_out[name] = t
        return t

    SB_LO, SB_HI = 16640, 229376
    cur = [SB_LO]
    cnt = [0]

    def sb(shape, dt, name=None):
        nbytes = int(np.prod(shape[1:])) * (2 if dt == BF16 else 4)
        nbytes = (nbytes + 63) // 64 * 64
        off = cur[0]
        cur[0] += nbytes
        assert cur[0] <= SB_HI, (name, cur[0])
        cnt[0] += 1
        return nc.alloc_sbuf_tensor_at(f"{name or 't'}_{cnt[0]}", list(shape), dt, offset=off)

    pcnt = [0]

    def psum(shape, dt=F32):
        pcnt[0] += 1
        return nc.alloc_psum_tensor(f"ps{pcnt[0]}", list(shape), dt)

    prew = sb([128, KC], F32, "prew")
    convw = sb([128, 24, 4], F32, "convw")
    convb = sb([128, 24], F32, "convb")
    hv = sb([128, 3, 32], F32, "hv")
    lam_sb = sb([128, 4, 64], F32, "lam")
    subw = sb([128, 128], F32, "subw")
    pmask = sb([128, 1], F32, "pmask")
    pbias = sb([128, 1], F32, "pbias")
    subc = sb([128, 1], F32, "subc")
    trif = sb([128, 128], F32, "trif")
    negm = sb([128, 128], F32, "negm")
    trib = sb([128, 128], BF16, "trib")
    mask2 = sb([128, 2, 256], BF16, "mask2")
    identb = sb([128, 128], BF16, "identb")
    onesf = sb([128, 128], F32, "onesf")
    A_b = sb([128, 32], F32, "A_b")
    neglam = sb([128, 1], F32, "neglam")
    lamtmp = sb([128, 4], F32, "lamtmp")
    lamjunk = sb([128, 64], F32, "lamjunk")
    cvec = sb([128, 4], F32, "cvec")
    subw8 = sb([128, 128], F32, "subw8")
    EPSB = cvec[:, 0:1]
    ONEB = cvec[:, 1:2]
    A("pool", lambda e: e.memset(cvec[:, 0:1], EPS), writes=["cvec"])
    A("pool", lambda e: e.memset(cvec[:, 1:2], 1.0), writes=["cvec"])

    consts = [(prew, prew_d), (convw, convw_d), (convb, convb_d), (hv, hv_d), (lam_sb, lam_d),
              (subw, subw_d), (pmask, pmask_d), (pbias, pbias_d), (subc, subc_d), (trif, trif_d), (negm, negm_d), (trib, trib_d), (mask2, mask2_d),
              (identb, identb_d), (onesf, onesf_d)]
    for t, d in consts:
        A("sp", (lambda t=t, d=d: lambda e: e.dma_start(out=t[:], in_=d[:]))(), writes=["consts"], dma="consts")

    A("dve", lambda e: e.tensor_scalar(out=subc[:, :], in0=subc[:, :], scalar1=1.0 - LAM_INIT, scalar2=None, op0=ALU.mult),
      reads=["consts"], writes=["subc8"])
    A("act", lambda e: e.activation(out=A_b[:, :], in_=hv[:, 1, :], func=AF.Exp), reads=["consts"], writes=["A_b0"])
    A("dve", lambda e: e.tensor_scalar(out=A_b[:, :], in0=A_b[:, :], scalar1=-1.0, scalar2=None, op0=ALU.mult),
      reads=["A_b0"], writes=["A_b"])
    for i in range(2):
        A("dve", (lambda i=i: lambda e: e.scalar_tensor_tensor(
            out=lamjunk[:, :], in0=lam_sb[:, 2 * i, :], scalar=1.0, in1=lam_sb[:, 2 * i + 1, :],
            op0=ALU.mult, op1=ALU.mult, accum_out=lamtmp[:, i:i + 1]))(),
          reads=["consts"], writes=["lamjunk", f"lamt{i}"])
    A("act", lambda e: e.activation(out=lamtmp[:, 2:4], in_=lamtmp[:, 0:2], func=AF.Exp),
      reads=["lamt0", "lamt1"], writes=["lame"])
    A("dve", lambda e: e.tensor_tensor(out=neglam[:, :], in0=lamtmp[:, 3:4], in1=lamtmp[:, 2:3], op=ALU.subtract),
      reads=["lame"], writes=["neglam0"])
    A("dve", lambda e: e.tensor_scalar(out=neglam[:, :], in0=neglam[:, :], scalar1=-LAM_INIT, scalar2=None, op0=ALU.add),
      reads=["neglam0"], writes=["neglam"])

    cast_done = set()
    cast_i = [0]

    def cast(name):
        if name in cast_done:
            return
        cast_done.add(name)
        key = f"cast{cast_i[0] % ncast}"
        cast_i[0] += 1
        if name == "dt":
            src, dst = wdt_d[:, :], wbdt
        elif name.startswith("o"):
            j = int(name[1:])
            src, dst = wout_d[:, j * 512:(j + 1) * 512], wbo[j]
        else:
            off, w = [(o, w) for (n, o, w) in PANELS if n == name][0]
            src, dst = win_d[:, off:off + w], wbp[name]
        A("pool", lambda e: e.dma_start(out=dst.rearrange("p kc n -> kc p n"),
                                        in_=src.rearrange("(kc p) n -> kc p n", p=128)),
          reads=[key], writes=[key, "wb_" + name], dma=key)

    P1 = cur[0]
    xins = [sb([128, D], F32, f"xin{i}") for i in range(2)]
    xn = sb([128, D], BF16, "xn")
    hT = sb([128, KC, TT], BF16, "hT")
    wpan = [sb([128, KC, 512], BF16, f"wpan{i}") for i in range(2)]
    wdt = sb([128, KC, 32], BF16, "wdt")
    stat = sb([128, 8], F32, "stat")
    halo = sb([128, 24, 3], F32, "halo")
    utmp = [sb([128, TT + 3], F32, f"utmp{i}") for i in range(2)]
    cacc = [sb([128, TT], F32, f"cacc{i}") for i in range(2)]
    fm = sb([128, 6, TT], BF16, "fm")
    xs_tok = sb([128, 4, 512], BF16, "xs_tok")
    b_tok = sb([128, 4, 128], BF16, "b_tok")
    sz = sb([128, 4, 512], BF16, "sz")
    y_tok = sb([128, 4, 512], BF16, "y_tok")
    dt_sb = sb([128, 4, 32], F32, "dt_sb")
    a_sb = sb([128, 4, 32], F32, "a_sb")
    cs_sb = sb([128, 32], F32, "cs_sb")
    ecs = sb([128, 32], F32, "ecs")
    dsd = sb([128, 32], F32, "dsd")
    dec = sb([128, 32], F32, "dec")
    dmat = sb([128, 4, 128], F32, "dmat")
    Lt = sb([128, 4, 128], BF16, "Lt")
    cbt = sb([128, 128], BF16, "cbt")
    Gm = sb([128, 8, 128], BF16, "Gm")
    Xb = sb([128, 512], BF16, "Xb")
    Xd = sb([128, 512], BF16, "Xd")
    xsD = sb([128, 512], BF16, "xsD")
    t1 = sb([128, 512], F32, "t1")
    t2 = sb([128, 512], F32, "t2")
    junkf = sb([128, 512], BF16, "junkf")
    Sin = sb([128, NG, 512], F32, "Sin")
    Sin_bf = sb([128, NG, 512], BF16, "Sin_bf")
    snw = sb([128, 512], F32, "snw")
    stg = [sb([128, 512], BF16, f"stg{i}") for i in range(2)]
    yTs = sb([128, 4, TT], BF16, "yTs")

    tr_f32 = [psum([128, 512]) for _ in range(2)]
    ps_trs = [t[:, :].bitcast(BF16) for t in tr_f32]
    pbig = [psum([128, 1024]) for _ in range(3)]
    banks = [pbig[i // 2][:, (i % 2) * 512:(i % 2 + 1) * 512] for i in range(6)]
    ps_mm = banks[0:2]
    ps_R = banks[2][:, :].rearrange("p (r l) -> p r l", r=4)
    ps_misc = banks[3]
    ps_y = banks[4]
    ps_yo = banks[5]
    ps_sc = banks[5]
    cstot = sb([128, 40], F32, "cstot")

    A("pool", lambda e: e.memset(halo[:, :, :], 0.0), writes=["halo"])
    A("pool", lambda e: e.memset(Sin[:, :, :], 0.0), writes=["Sin"])
    A("pool", lambda e: e.memset(Sin_bf[:, :, :], 0.0), writes=["Sin_bf"])

    cast("dt")
    A("sp", lambda e: e.dma_start(out=wdt[:, :, :], in_=wbdt[:, :, :]), reads=["wb_dt"], writes=["wdt"], dma="wdt")

    tile_panels_prefix = ["xs", "bc"], ["k", "v"]
    pan_i = [0]
    cur_tile = [0]
    stg_i = [0]
    tr_i = [0]

    later = [f"z{i}" for i in range(4)] + [f"q{i}" for i in range(4)] + [f"za{i}" for i in range(4)] + [f"o{i}" for i in range(8)]

    def load_panel(name):
        slot = pan_i[0] % 2
        pan_i[0] += 1
        w = [w for (n, _, w) in PANELS if n == name][0]
        cast(name)
        if cur_tile[0] >= 1 and later and pan_i[0] % 2 == 0:
            cast(later.pop(0))
        A("sp", lambda e: e.dma_start(out=wpan[slot][:, :, 0:w], in_=wbp[name][:, :, :]),
          reads=["wb_" + name], writes=[f"wpan{slot}"], dma=f"wpan{slot}")
        return slot, w

    def evac_store(src_ap, dst_ap, eng, scale=None, func=None):
        si = stg_i[0] % 2
        stg_i[0] += 1
        st = stg[si]
        if eng == "act":
            A("act", lambda e: e.activation(out=st[:, :], in_=src_ap, func=func or AF.Copy,
                                            scale=1.0 if scale is None else scale),
              reads=[src_key[0]], writes=[f"stg{si}"])
        else:
            A("dve", lambda e: e.tensor_copy(out=st[:, :], in_=src_ap), reads=[src_key[0]], writes=[f"stg{si}"])
        A("pool", lambda e: e.dma_start(out=dst_ap, in_=st[:, :]), reads=[f"stg{si}"], dma=f"stg{si}")

    src_key = [None]
    mm_i = [0]

    def transposes4(srcs, evac):
        h = tr_i[0] % 2
        tr_i[0] += 1
        for j, (ap, rk) in enumerate(srcs):
            A("pe", (lambda ap=ap, j=j: lambda e: e.transpose(ps_trs[h][:, j * 128:(j + 1) * 128], ap, identb[:, :]))(),
              reads=rk + ["consts"], writes=[f"ps_tr{h}"])
        evac(ps_trs[h][:, 0:512], f"ps_tr{h}")

    for tile in range(ntiles):
        own = tile >= 4
        t0 = tile * TT
        cur_tile[0] = tile
        for s in range(4):
            r0 = t0 + s * 128
            xi = (tile * 4 + s) % 2
            xin = xins[xi]
            xk = f"xin{xi}"
            if tile == 0 and s == 0:
                A("sp", lambda e: e.dma_start(out=xin[:, :], in_=x_d[r0:r0 + 128, :]), writes=[xk], dma=xk)
            nxt = tile * 4 + s + 1
            if nxt < ntiles * 4:
                xo = xins[nxt % 2]
                rn = nxt * 128
                A("sp", lambda e: e.dma_start(out=xo[:, :], in_=x_d[rn:rn + 128, :]), writes=[f"xin{nxt % 2}"], dma=f"xin{nxt % 2}")
            A("act", lambda e: e.activation(out=xn[:, :], in_=xin[:, :], func=AF.Square, accum_out=stat[:, 0:1]),
              reads=[xk], writes=["xn", "stat0"])
            A("act", lambda e: e.activation(out=stat[:, 1:2], in_=stat[:, 0:1], func=AF.Sqrt, scale=1.0 / D, bias=EPSB[:, :]),
              reads=["stat0", "cvec"], writes=["stat1"])
            P.mark("n_sqrt")
            A("dve", lambda e: e.reciprocal(out=stat[:, 2:3], in_=stat[:, 1:2]), reads=["stat1"], writes=["stat2"])
            A("dve", lambda e: e.tensor_scalar(out=xn[:, :], in0=xin[:, :], scalar1=stat[:, 2:3], scalar2=None, op0=ALU.mult),
              reads=[xk, "stat2"], writes=["xn"])
            P.mark("n_scale")
            for q4 in range(8):
                def ev(ps_ap, key, q4=q4, s=s):
                    eng = "dve" if q4 % 2 == 0 else "pool"
                    eng = "dve"
                    A(eng, lambda e: e.tensor_tensor(
                        out=hT[:, q4 * 4:(q4 + 1) * 4, s * 128:(s + 1) * 128],
                        in0=ps_ap.rearrange("p (j t) -> p j t", j=4),
                        in1=prew[:, q4 * 4:(q4 + 1) * 4].unsqueeze(2).to_broadcast([128, 4, 128]), op=ALU.mult),
                      reads=[key, "consts"], writes=["hT"])
                transposes4([(xn[:, (q4 * 4 + j) * 128:(q4 * 4 + j + 1) * 128], ["xn"]) for j in range(4)], ev)
                P.mark(f"n_tr{s}_{q4}")

        P.mark(f"norm{tile}")
        for s in range(4):
            for kc in range(KC):
                A("pe", (lambda kc=kc, s=s: lambda e: e.matmul(ps_misc[:, 192:224], lhsT=hT[:, kc, s * 128:(s + 1) * 128],
                                                            rhs=wdt[:, kc, :], start=(kc == 0), stop=(kc == KC - 1)))(),
                  reads=["hT", "wdt"], writes=["ps_misc"])
            A("dve", (lambda s=s: lambda e: e.tensor_tensor(out=dt_sb[:, s, :], in0=ps_misc[:, 192:224], in1=hv[:, 0, :], op=ALU.add))(),
              reads=["ps_misc", "consts"], writes=["dt_sb"])
        A("act", lambda e: e.activation(out=dt_sb[:, :, :], in_=dt_sb[:, :, :], func=AF.Exp), reads=["dt_sb"], writes=["dt_sb"])
        A("act", lambda e: e.activation(out=dt_sb[:, :, :], in_=dt_sb[:, :, :], func=AF.Ln, bias=ONEB[:, :]), reads=["dt_sb", "cvec"], writes=["dt_sb"])
        A("dve", lambda e: e.tensor_tensor(out=a_sb[:, :, :], in0=dt_sb[:, :, :],
                                           in1=A_b[:, :].unsqueeze(1).to_broadcast([128, 4, 32]), op=ALU.mult),
          reads=["dt_sb", "A_b"], writes=["a_sb"])

        P.mark(f"dt{tile}")
        for g in range(NG):
            for pname, nchunk, c0 in ((f"xs{g}", 4, 0), (f"bc{g}", 2, 4)):
                slot, w = load_panel(pname)
                for cc in range(nchunk):
                    bank = mm_i[0] % 2
                    mm_i[0] += 1
                    for kc in range(KC):
                        A("pe", (lambda kc=kc, cc=cc, slot=slot, bank=bank: lambda e: e.matmul(
                            ps_mm[bank][:, :], lhsT=wpan[slot][:, kc, cc * 128:(cc + 1) * 128], rhs=hT[:, kc, :],
                            start=(kc == 0), stop=(kc == KC - 1)))(),
                          reads=[f"wpan{slot}", "hT"], writes=[f"ps_mm{bank}"])
                    ch = g * 6 + c0 + cc
                    u = utmp[cc % 2]
                    ca = cacc[cc % 2]
                    uk, ck = f"utmp{cc % 2}", f"cacc{cc % 2}"
                    A("act", (lambda u=u, bank=bank: lambda e: e.copy(out=u[:, 3:TT + 3], in_=ps_mm[bank][:, :]))(),
                      reads=[f"ps_mm{bank}"], writes=[uk])
                    A("pool", (lambda u=u, ch=ch: lambda e: e.tensor_copy(out=u[:, 0:3], in_=halo[:, ch, :]))(),
                      reads=["halo"], writes=[uk + "h"])
                    A("dve", (lambda u=u, ca=ca, ch=ch: lambda e: e.tensor_scalar(
                        out=ca[:, :], in0=u[:, 0:TT], scalar1=convw[:, ch, 0:1], scalar2=None, op0=ALU.mult))(),
                      reads=[uk, uk + "h", "consts"], writes=[ck])
                    for j in range(1, 4):
                        A("dve", (lambda u=u, ca=ca, ch=ch, j=j: lambda e: e.scalar_tensor_tensor(
                            out=ca[:, :], in0=u[:, j:j + TT], scalar=convw[:, ch, j:j + 1], in1=ca[:, :],
                            op0=ALU.mult, op1=ALU.add))(),
                          reads=[uk, uk + "h", ck, "consts"], writes=[ck])
                    A("pool", (lambda u=u, ch=ch: lambda e: e.tensor_copy(out=halo[:, ch, :], in_=u[:, TT:TT + 3]))(),
                      reads=[uk], writes=["halo"])
                    A("act", (lambda ca=ca, ch=ch, c0=c0, cc=cc: lambda e: e.activation(
                        out=fm[:, c0 + cc, :], in_=ca[:, :], func=AF.Silu, bias=convb[:, ch:ch + 1]))(),
                      reads=[ck, "consts"], writes=[f"fm{c0 + cc}"])
            P.mark(f"conv{tile}_{g}")
            for s in range(4):
                def ev(ps_ap, key, s=s):
                    A("act", lambda e: e.copy(out=xs_tok[:, s, :], in_=ps_ap), reads=[key], writes=["xs_tok"])
                transposes4([(fm[:, c, s * 128:(s + 1) * 128], [f"fm{c}"]) for c in range(4)], ev)

            def evb(ps_ap, key):
                A("act", lambda e: e.copy(out=b_tok[:, :, :], in_=ps_ap.rearrange("p (s n) -> p s n", s=4)),
                  reads=[key], writes=["b_tok"])
            transposes4([(fm[:, 4, s * 128:(s + 1) * 128], ["fm4"]) for s in range(4)], evb)

            P.mark(f"tok{tile}_{g}")
            if own:
                A("sp", lambda e: e.dma_start(out=snw[:, :], in_=snw_d[:, g * 512:(g + 1) * 512]), writes=["snw"], dma="snw")
                slot, w = load_panel(f"z{g}")
                for s in range(4):
                    bank = mm_i[0] % 2
                    mm_i[0] += 1
                    for kc in range(KC):
                        A("pe", (lambda kc=kc, s=s, slot=slot, bank=bank: lambda e: e.matmul(
                            ps_mm[bank][:, :], lhsT=hT[:, kc, s * 128:(s + 1) * 128], rhs=wpan[slot][:, kc, :],
                            start=(kc == 0), stop=(kc == KC - 1)))(),
                          reads=[f"wpan{slot}", "hT"], writes=[f"ps_mm{bank}"])
                    A("act", (lambda s=s, bank=bank: lambda e: e.activation(out=sz[:, s, :], in_=ps_mm[bank][:, :], func=AF.Silu))(),
                      reads=[f"ps_mm{bank}"], writes=["sz"])

            P.mark(f"z{tile}_{g}")
            h0 = g * 8
            for s in range(4):
                A("pe", (lambda s=s: lambda e: e.matmul(ps_misc[:, 128:136], lhsT=trif[:, :], rhs=a_sb[:, s, h0:h0 + 8], start=True, stop=True))(),
                  reads=["a_sb", "consts"], writes=["ps_misc"])
                A("pe", (lambda s=s: lambda e: e.matmul(ps_misc[:, 136:144], lhsT=onesf[:, :], rhs=a_sb[:, s, h0:h0 + 8], start=True, stop=True))(),
                  reads=["a_sb", "consts"], writes=["ps_misc"])
                A("dve", lambda e: e.tensor_copy(out=cstot[:, 0:16], in_=ps_misc[:, 128:144]), reads=["ps_misc"], writes=["cs_sb"])
                A("act", lambda e: e.activation(out=dec[:, 0:8], in_=cstot[:, 8:16], func=AF.Exp), reads=["cs_sb"], writes=["dec"])
                A("dve", lambda e: e.tensor_tensor(out=dsd[:, 0:8], in0=cstot[:, 8:16], in1=cstot[:, 0:8], op=ALU.subtract),
                  reads=["cs_sb"], writes=["dsd0"])
                A("act", lambda e: e.activation(out=dsd[:, 0:8], in_=dsd[:, 0:8], func=AF.Exp), reads=["dsd0"], writes=["dsd"])
                A("dve", (lambda s=s: lambda e: e.tensor_tensor(
                    out=Xb[:, :].rearrange("p (r d) -> p r d", r=8), in0=xs_tok[:, s, :].rearrange("p (r d) -> p r d", r=8),
                    in1=dt_sb[:, s, h0:h0 + 8].unsqueeze(2).to_broadcast([128, 8, 64]), op=ALU.mult))(),
                  reads=["xs_tok", "dt_sb"], writes=["Xb"])
                A("dve", lambda e: e.tensor_tensor(
                    out=Xd[:, :].rearrange("p (r d) -> p r d", r=8), in0=Xb[:, :].rearrange("p (r d) -> p r d", r=8),
                    in1=dsd[:, 0:8].unsqueeze(2).to_broadcast([128, 8, 64]), op=ALU.mult),
                  reads=["Xb", "dsd"], writes=["Xd"])
                if own:
                    A("act", lambda e: e.activation(out=ecs[:, 0:8], in_=cstot[:, 0:8], func=AF.Exp), reads=["cs_sb"], writes=["ecs"])
                    A("pe", (lambda s=s: lambda e: e.matmul(ps_misc[:, 0:128], lhsT=fm[:, 4, s * 128:(s + 1) * 128],
                                                         rhs=fm[:, 5, s * 128:(s + 1) * 128], start=True, stop=True))(),
                      reads=["fm4", "fm5"], writes=["ps_misc"])
                    A("dve", lambda e: e.tensor_tensor(out=cbt[:, :], in0=ps_misc[:, 0:128], in1=trib[:, :], op=ALU.mult),
                      reads=["ps_misc", "consts"], writes=["cbt"])
                    A("pool", (lambda s=s: lambda e: e.tensor_tensor(
                        out=xsD[:, :].rearrange("p (r d) -> p r d", r=8), in0=xs_tok[:, s, :].rearrange("p (r d) -> p r d", r=8),
                        in1=hv[:, 2, h0:h0 + 8].unsqueeze(2).to_broadcast([128, 8, 64]), op=ALU.mult))(),
                      reads=["xs_tok", "consts"], writes=["xsD"])
                    for hh in range(2):
                        for r in range(4):
                            rr = hh * 4 + r
                            A("pe", (lambda s=s, r=r, rr=rr: lambda e: e.matmul(
                                ps_R[:, r, :], lhsT=a_sb[:, s, h0 + rr:h0 + rr + 1].to_broadcast([128, 128]), rhs=trif[:, :],
                                start=True, stop=True))(),
                              reads=["a_sb", "consts"], writes=["ps_R"])
                        for r in range(4):
                            rr = hh * 4 + r
                            A("dve", (lambda r=r, rr=rr: lambda e: e.scalar_tensor_tensor(
                                out=dmat[:, r, :], in0=ps_R[:, r, :], scalar=cstot[:, rr:rr + 1], in1=negm[:, :],
                                op0=ALU.subtract, op1=ALU.add))(),
                              reads=["ps_R", "cs_sb", "consts"], writes=["dmat"])
                        A("act", lambda e: e.activation(out=Lt[:, :, :], in_=dmat[:, :, :], func=AF.Exp), reads=["dmat"], writes=["Lt"])
                        A("dve", (lambda hh=hh: lambda e: e.tensor_tensor(
                            out=Gm[:, hh * 4:(hh + 1) * 4, :], in0=Lt[:, :, :],
                            in1=cbt[:, :].unsqueeze(1).to_broadcast([128, 4, 128]), op=ALU.mult))(),
                          reads=["Lt", "cbt"], writes=["Gm"])
                    A("pe", lambda e: e.matmul(ps_y[:, :], lhsT=identb[:, :], rhs=xsD[:, :], start=True, stop=False),
                      reads=["xsD", "consts"], writes=["ps_y"])
                    for rr in range(8):
                        A("pe", (lambda rr=rr: lambda e: e.matmul(
                            ps_y[:, rr * 64:(rr + 1) * 64], lhsT=Gm[:, rr, :], rhs=Xb[:, rr * 64:(rr + 1) * 64],
                            start=False, stop=(rr == 7)))(),
                          reads=["Gm", "Xb"], writes=["ps_y"])
                    A("pe", (lambda s=s: lambda e: e.matmul(ps_yo[:, :], lhsT=fm[:, 5, s * 128:(s + 1) * 128],
                                                         rhs=Sin_bf[:, g, :], start=True, stop=True))(),
                      reads=["fm5", "Sin_bf"], writes=["ps_ys"])
                    A("dve", lambda e: e.tensor_tensor(
                        out=t1[:, :].rearrange("p (r d) -> p r d", r=8), in0=ps_yo[:, :].rearrange("p (r d) -> p r d", r=8),
                        in1=ecs[:, 0:8].unsqueeze(2).to_broadcast([128, 8, 64]), op=ALU.mult),
                      reads=["ps_ys", "ecs"], writes=["t1"])
                    A("dve", lambda e: e.tensor_tensor(out=t2[:, :], in0=ps_y[:, :], in1=t1[:, :], op=ALU.add),
                      reads=["ps_y", "t1"], writes=["t2"])
                    A("dve", (lambda s=s: lambda e: e.tensor_tensor(out=t1[:, :], in0=t2[:, :], in1=sz[:, s, :], op=ALU.mult))(),
                      reads=["t2", "sz"], writes=["t1"])
                    A("act", lambda e: e.activation(out=junkf[:, :], in_=t1[:, :], func=AF.Square, accum_out=stat[:, 3:4]),
                      reads=["t1"], writes=["junkf", "stat3"])
                    A("act", lambda e: e.activation(out=stat[:, 4:5], in_=stat[:, 3:4], func=AF.Sqrt, scale=1.0 / 512, bias=EPSB[:, :]),
                      reads=["stat3", "cvec"], writes=["stat4"])
                    A("dve", lambda e: e.reciprocal(out=stat[:, 5:6], in_=stat[:, 4:5]), reads=["stat4"], writes=["stat5"])
                    A("dve", (lambda s=s: lambda e: e.scalar_tensor_tensor(
                        out=y_tok[:, s, :], in0=t1[:, :], scalar=stat[:, 5:6], in1=snw[:, :],
                        op0=ALU.mult, op1=ALU.mult))(),
                      reads=["t1", "stat5", "snw"], writes=["y_tok"])
                A("pe", (lambda s=s: lambda e: e.matmul(ps_sc[:, :], lhsT=b_tok[:, s, :], rhs=Xd[:, :], start=True, stop=True))(),
                  reads=["b_tok", "Xd"], writes=["ps_ys"])
                A("dve", lambda e: e.tensor_tensor(
                    out=Sin[:, g, :].rearrange("p (r d) -> p r d", r=8), in0=Sin[:, g, :].rearrange("p (r d) -> p r d", r=8),
                    in1=dec[:, 0:8].unsqueeze(2).to_broadcast([128, 8, 64]), op=ALU.mult),
                  reads=["Sin", "dec"], writes=["Sin"])
                A("dve", lambda e: e.tensor_tensor(out=Sin[:, g, :], in0=Sin[:, g, :], in1=ps_sc[:, :], op=ALU.add),
                  reads=["Sin", "ps_ys"], writes=["Sin"])
                if tile == 3 and s == 3:
                    A("dve", lambda e: e.tensor_scalar(out=Sin[:, g, :], in0=Sin[:, g, :], scalar1=pmask[:, 0:1], scalar2=None, op0=ALU.mult),
                      reads=["Sin", "consts"], writes=["Sin"])
                if own or (tile == 3 and s == 3):
                    A("act", lambda e: e.copy(out=Sin_bf[:, g, :], in_=Sin[:, g, :]), reads=["Sin"], writes=["Sin_bf"])
            P.mark(f"ssd{tile}_{g}")
            if own:
                for c in range(4):
                    def ev(ps_ap, key, c=c):
                        A("act", lambda e: e.copy(out=yTs[:, c, :], in_=ps_ap), reads=[key], writes=["yTs"])
                    transposes4([(y_tok[:, s, c * 128:(c + 1) * 128], ["y_tok"]) for s in range(4)], ev)
                o0 = (tile - 4) * TT
                A("pool", (lambda g=g, o0=o0: lambda e: e.dma_start(
                    out=yT_d[g * 512:(g + 1) * 512, o0:o0 + TT].rearrange("(c p) t -> p c t", p=128), in_=yTs[:, :, :]))(),
                  reads=["yTs"], dma="yTs")

        P.mark(f"groups{tile}")
        for i in range(4):
            slot, w = load_panel(f"k{i}")
            for cc in range(4):
                bank = mm_i[0] % 2
                mm_i[0] += 1
                for kc in range(KC):
                    A("pe", (lambda kc=kc, cc=cc, slot=slot, bank=bank: lambda e: e.matmul(
                        ps_mm[bank][:, :], lhsT=wpan[slot][:, kc, cc * 128:(cc + 1) * 128], rhs=hT[:, kc, :],
                        start=(kc == 0), stop=(kc == KC - 1)))(),
                      reads=[f"wpan{slot}", "hT"], writes=[f"ps_mm{bank}"])
                src_key[0] = f"ps_mm{bank}"
                evac_store(ps_mm[bank][:, :], kT_d[i * 4 + cc, :, t0:t0 + TT], "dve" if cc % 2 else "act")
        for i in range(4):
            slot, w = load_panel(f"v{i}")
            for s in range(4):
                bank = mm_i[0] % 2
                mm_i[0] += 1
                for kc in range(KC):
                    A("pe", (lambda kc=kc, s=s, slot=slot, bank=bank: lambda e: e.matmul(
                        ps_mm[bank][:, :], lhsT=hT[:, kc, s * 128:(s + 1) * 128], rhs=wpan[slot][:, kc, :],
                        start=(kc == 0), stop=(kc == KC - 1)))(),
                      reads=[f"wpan{slot}", "hT"], writes=[f"ps_mm{bank}"])
                src_key[0] = f"ps_mm{bank}"
                evac_store(ps_mm[bank][:, :], v_d[t0 + s * 128:t0 + (s + 1) * 128, i * 512:(i + 1) * 512], "dve" if s % 2 else "act")
        if own:
            o0 = (tile - 4) * TT
            for i in range(4):
                slot, w = load_panel(f"q{i}")
                for cc in range(4):
                    bank = mm_i[0] % 2
                    mm_i[0] += 1
                    for kc in range(KC):
                        A("pe", (lambda kc=kc, cc=cc, slot=slot, bank=bank: lambda e: e.matmul(
                            ps_mm[bank][:, :], lhsT=wpan[slot][:, kc, cc * 128:(cc + 1) * 128], rhs=hT[:, kc, :],
                            start=(kc == 0), stop=(kc == KC - 1)))(),
                          reads=[f"wpan{slot}", "hT"], writes=[f"ps_mm{bank}"])
                    src_key[0] = f"ps_mm{bank}"
                    evac_store(ps_mm[bank][:, :], qT_d[i * 4 + cc, :, o0:o0 + TT], "act", scale=0.125)
            for i in range(4):
                slot, w = load_panel(f"za{i}")
                for cc in range(4):
                    bank = mm_i[0] % 2
                    mm_i[0] += 1
                    for kc in range(KC):
                        A("pe", (lambda kc=kc, cc=cc, slot=slot, bank=bank: lambda e: e.matmul(
                            ps_mm[bank][:, :], lhsT=wpan[slot][:, kc, cc * 128:(cc + 1) * 128], rhs=hT[:, kc, :],
                            start=(kc == 0), stop=(kc == KC - 1)))(),
                          reads=[f"wpan{slot}", "hT"], writes=[f"ps_mm{bank}"])
                    src_key[0] = f"ps_mm{bank}"
                    evac_store(ps_mm[bank][:, :], zaT_d[i * 4 + cc, :, o0:o0 + TT], "act", func=AF.Silu)

    if dump:
        for nm, t, key in (("hT", hT, "hT"), ("xn", xn, "xn"), ("stat", stat, "stat2"), ("fm", fm, "fm0"), ("dt", dt_sb, "dt_sb"), ("a", a_sb, "a_sb"), ("xs_tok", xs_tok, "xs_tok"),
                           ("b_tok", b_tok, "b_tok"), ("Sin", Sin, "Sin"), ("y_tok", y_tok, "y_tok"), ("sz", sz, "sz"), ("Gm", Gm, "Gm"),
                           ("cs", cstot, "cs_sb"), ("t2", t2, "t2"), ("Xb", Xb, "Xb"), ("Xd", Xd, "Xd"), ("cbt", cbt, "cbt")):
            dd = ddump(nm, list(t.shape), t.dtype)
            A("pool", (lambda dd=dd, t=t: lambda e: e.dma_start(out=dd[:], in_=t[:]))(), reads=[key], dma="dump_" + nm, out=True)
    if not do_attn:
        P.emit()
        return nc
    P.barrier()
    cur[0] = P1
    for j in range(8):
        cast(f"o{j}")
    kTh = [sb([128, SEQ], BF16, f"kTh{i}") for i in range(2)]
    qTh = [sb([128, OWN], BF16, f"qTh{i}") for i in range(2)]
    Vh = [sb([128, 32, 128], BF16, f"Vh{i}") for i in range(2)]
    zaTh = [sb([128, OWN], BF16, f"zaTh{i}") for i in range(2)]
    Et = [[sb([128, 512], BF16, f"Et{m}{i}") for i in range(2)] for m in range(2)]
    onesb = sb([128, 128], BF16, "onesb")
    A("pool", lambda e: e.memset(onesb[:, :], 1.0), writes=["onesb"])
    Osb = [sb([128, 512], F32, f"Osb{m}") for m in range(2)]
    rec = [sb([128, 512], F32, f"rec{m}") for m in range(2)]
    sqb = sb([128, 512], F32, "sqb")
    rsb = sb([128, 512], F32, "rsb")
    yTa = [sb([128, 512], BF16, f"yTa{i}") for i in range(2)]
    Sb = [pbig[0][:, 0:512], pbig[0][:, 512:1024]]
    bO = [pbig[1][:, 0:512], pbig[1][:, 512:1024]]
    bR = [pbig[2][:, 0:512], pbig[2][:, 512:1024]]
    fin_i = [0]
    pending = []
    inflight = []
    ssb = tr_f32[0]

    def fin_math():
        if not pending:
            return
        hh, qq, bb = pending.pop(0)
        for m in range(2):
            A("dve", lambda e: e.reciprocal(out=rec[m][:, :], in_=rec[m][:, :]), reads=[f"rec{m}"], writes=[f"rec{m}"])
            A("dve", lambda e: e.tensor_tensor(out=Osb[m][:, :], in0=Osb[m][:, :], in1=rec[m][:, :], op=ALU.mult),
              reads=[f"Osb{m}", f"rec{m}"], writes=[f"Osb{m}"])
        A("dve", lambda e: e.scalar_tensor_tensor(out=Osb[0][:, :], in0=Osb[1][:, :], scalar=neglam[:, 0:1], in1=Osb[0][:, :],
                                                  op0=ALU.mult, op1=ALU.add),
          reads=["Osb0", "Osb1", "neglam"], writes=["Osb0"])
        A("pool", lambda e: e.tensor_tensor(out=sqb[:, :], in0=Osb[0][:, :], in1=Osb[0][:, :], op=ALU.mult), reads=["Osb0"], writes=["sqb"])
        inflight.append((hh, qq, bb))

    def fin_tail():
        if not inflight:
            return
        hh, qq, bb = inflight.pop(0)
        A("pe", lambda e: e.matmul(ssb[:, :], lhsT=onesf[:, :], rhs=sqb[:, :], start=True, stop=True),
          reads=["sqb", "consts"], writes=["ssb"])
        A("act", lambda e: e.activation(out=rsb[:, :], in_=ssb[:, :], func=AF.Sqrt, scale=1.0 / 128, bias=EPSB),
          reads=["ssb", "cvec"], writes=["rsb"])
        A("dve", lambda e: e.reciprocal(out=rsb[:, :], in_=rsb[:, :]), reads=["rsb"], writes=["rsb"])
        A("dve", lambda e: e.scalar_tensor_tensor(out=Osb[0][:, :], in0=Osb[0][:, :], scalar=subc[:, 0:1], in1=rsb[:, :],
                                                  op0=ALU.mult, op1=ALU.mult),
          reads=["Osb0", "rsb", "subc8"], writes=["Osb0"])
        yb = fin_i[0] % 2
        fin_i[0] += 1
        A("pool", lambda e: e.tensor_tensor(out=yTa[yb][:, :], in0=Osb[0][:, :], in1=zaTh[bb][:, qq * 512:(qq + 1) * 512], op=ALU.mult),
          reads=["Osb0", f"zaTh{bb}"], writes=[f"yTa{yb}"])
        A("pool", lambda e: e.dma_start(out=yT_d[2048 + hh * 128:2048 + (hh + 1) * 128, qq * 512:(qq + 1) * 512], in_=yTa[yb][:, :]),
          reads=[f"yTa{yb}"], dma=f"yTa{yb}")

    def flush_all():
        fin_math()
        fin_tail()

    for h in range(NH):
        b = h % 2
        A("sp", lambda e: e.dma_start(out=kTh[b][:, :], in_=kT_d[h, :, :]), writes=[f"kTh{b}"], dma=f"kTh{b}")
        A("sp", lambda e: e.dma_start(out=qTh[b][:, :], in_=qT_d[h, :, :]), writes=[f"qTh{b}"], dma=f"qTh{b}")
        A("sp", lambda e: e.dma_start(out=Vh[b][:, :, :], in_=v_d[:, h * 128:(h + 1) * 128].rearrange("(k p) e -> p k e", p=128)),
          writes=[f"Vh{b}"], dma=f"Vh{b}")
        A("sp", lambda e: e.dma_start(out=zaTh[b][:, :], in_=zaT_d[h, :, :]), writes=[f"zaTh{b}"], dma=f"zaTh{b}")
        for qc in range(4):
            kb_diag = 16 + 4 * qc
            npair = (kb_diag + 4) // 2

            nkb = kb_diag + 4

            def geom(kb):
                di = kb - kb_diag
                if di < 0:
                    return di, 0
                return di, (di - (di % 2)) * 128

            def qk(kb):
                di, n0 = geom(kb)
                t = kb % 2
                for m in range(2):
                    et = Et[m][t]
                    bank = Sb[m]
                    key = f"bS{m}"
                    A("pe", lambda e: e.matmul(
                        bank[:, n0:512], lhsT=kTh[b][m * 64:(m + 1) * 64, kb * 128:(kb + 1) * 128],
                        rhs=qTh[b][m * 64:(m + 1) * 64, qc * 512 + n0:(qc + 1) * 512], start=True, stop=True),
                      reads=[f"kTh{b}", f"qTh{b}"], writes=[key])
                    if kb < 16:
                        A("act", lambda e: e.activation(out=et[:, n0:512], in_=bank[:, n0:512], func=AF.Exp, bias=pbias[:, 0:1]),
                          reads=[key, "consts"], writes=[f"Et{m}{t}"])
                    else:
                        A("act", lambda e: e.activation(out=et[:, n0:512], in_=bank[:, n0:512], func=AF.Exp),
                          reads=[key], writes=[f"Et{m}{t}"])
                    if di >= 0:
                        A("dve", lambda e: e.tensor_tensor(out=et[:, n0:n0 + 256], in0=et[:, n0:n0 + 256], in1=mask2[:, t, :], op=ALU.mult),
                          reads=[f"Et{m}{t}", "consts"], writes=[f"Et{m}{t}"])

            def pv(kb):
                di, n0 = geom(kb)
                t = kb % 2
                for m in range(2):
                    et = Et[m][t]
                    A("pe", lambda e: e.matmul(bO[m][:, n0:512], lhsT=Vh[b][:, kb, :], rhs=et[:, n0:512],
                                               start=(kb == 0), stop=(kb == nkb - 1)),
                      reads=[f"Et{m}{t}", f"Vh{b}"], writes=[f"bO{m}"])
                    A("pe", lambda e: e.matmul(bR[m][:, n0:512], lhsT=onesb[:, :], rhs=et[:, n0:512],
                                               start=(kb == 0), stop=(kb == nkb - 1)),
                      reads=[f"Et{m}{t}", "onesb"], writes=[f"bR{m}"])

            qk(0)
            for kb in range(nkb):
                if kb + 1 < nkb:
                    qk(kb + 1)
                pv(kb)
                if kb == 1:
                    fin_math()
                if kb == 9:
                    fin_tail()
            A("act", lambda e: e.copy(out=Osb[0][:, :], in_=bO[0][:, :]), reads=["bO0"], writes=["Osb0"])
            A("dve", lambda e: e.tensor_copy(out=Osb[1][:, :], in_=bO[1][:, :]), reads=["bO1"], writes=["Osb1"])
            A("act", lambda e: e.copy(out=rec[0][:, :], in_=bR[0][:, :]), reads=["bR0"], writes=["rec0"])
            A("dve", lambda e: e.tensor_copy(out=rec[1][:, :], in_=bR[1][:, :]), reads=["bR1"], writes=["rec1"])
            pending.append((h, qc, b))
    flush_all()

    if not do_out:
        P.emit()
        return nc
    P.barrier()
    cur[0] = P1
    yT = sb([128, KC, TT], BF16, "yT")
    wpo = [sb([128, KC, 512], BF16, f"wpo{i}") for i in range(2)]
    out_sb = [sb([128, D], F32, f"out_sb{i}") for i in range(4)]
    postw = sb([128, D], F32, "postw")
    xres = sb([128, D], F32, "xres")
    ssq = sb([128, 4, 8], F32, "ssq")
    st4 = sb([128, 8], F32, "st4")
    junk4 = sb([128, 512], BF16, "junk4")
    A("sp", lambda e: e.dma_start(out=postw[:, :], in_=postw_d[:, :]), writes=["postw"], dma="postw")
    po_i = [0]
    for tile in range(4):
        o0 = tile * TT
        for s in range(4):
            A("sp", lambda e: e.dma_start(out=yT[:, :, s * 128:(s + 1) * 128],
                                          in_=yT_d[:, o0 + s * 128:o0 + (s + 1) * 128].rearrange("(kc p) t -> p kc t", p=128)),
              writes=[f"yT{s}"], dma=f"yT{s}")
        for n in range(8):
            slot = po_i[0] % 2
            po_i[0] += 1
            A("sp", (lambda n=n, slot=slot: lambda e: e.dma_start(out=wpo[slot][:, :, :], in_=wbo[n][:, :, :]))(),
              reads=[f"wb_o{n}"], writes=[f"wpo{slot}"], dma=f"wpo{slot}")
            for s in range(4):
                bank = mm_i[0] % 2
                mm_i[0] += 1
                for kc in range(KC):
                    A("pe", (lambda kc=kc, s=s, slot=slot, bank=bank: lambda e: e.matmul(
                        ps_mm[bank][:, :], lhsT=yT[:, kc, s * 128:(s + 1) * 128], rhs=wpo[slot][:, kc, :],
                        start=(kc == 0), stop=(kc == KC - 1)))(),
                      reads=[f"wpo{slot}", f"yT{s}"], writes=[f"ps_mm{bank}"])
                A("dve", (lambda s=s, n=n, bank=bank: lambda e: e.tensor_copy(out=out_sb[s][:, n * 512:(n + 1) * 512], in_=ps_mm[bank][:, :]))(),
                  reads=[f"ps_mm{bank}"], writes=[f"out_sb{s}"])
                A("act", (lambda s=s, n=n: lambda e: e.activation(out=junk4[:, :], in_=out_sb[s][:, n * 512:(n + 1) * 512], func=AF.Square,
                                                                accum_out=ssq[:, s, n:n + 1]))(),
                  reads=[f"out_sb{s}"], writes=["junk4", f"ssq{s}"])
                if n == 7:
                    r0 = OWN + o0 + s * 128
                    A("sp", (lambda r0=r0: lambda e: e.dma_start(out=xres[:, :], in_=x_d[r0:r0 + 128, :]))(), writes=["xres"], dma="xres")
                    A("dve", (lambda s=s: lambda e: e.tensor_reduce(out=st4[:, 0:1], in_=ssq[:, s, :], axis=AX.X, op=ALU.add))(),
                      reads=[f"ssq{s}"], writes=["st40"])
                    A("act", lambda e: e.activation(out=st4[:, 1:2], in_=st4[:, 0:1], func=AF.Sqrt, scale=1.0 / D, bias=EPSB),
                      reads=["st40", "cvec"], writes=["st41"])
                    A("dve", lambda e: e.reciprocal(out=st4[:, 2:3], in_=st4[:, 1:2]), reads=["st41"], writes=["st42"])
                    A("dve", (lambda s=s: lambda e: e.scalar_tensor_tensor(out=out_sb[s][:, :], in0=out_sb[s][:, :], scalar=st4[:, 2:3],
                                                                         in1=postw[:, :], op0=ALU.mult, op1=ALU.mult))(),
                      reads=[f"out_sb{s}", "st42", "postw"], writes=[f"out_sb{s}"])
                    A("pool", (lambda s=s: lambda e: e.tensor_tensor(out=out_sb[s][:, :], in0=out_sb[s][:, :], in1=xres[:, :], op=ALU.add))(),
                      reads=[f"out_sb{s}", "xres"], writes=[f"out_sb{s}"])
                    A("pool", (lambda s=s, o0=o0: lambda e: e.dma_start(out=out_d[o0 + s * 128:o0 + (s + 1) * 128, :], in_=out_sb[s][:, :]))(),
                      reads=[f"out_sb{s}"], dma=f"out_sb{s}", out=True)

    P.emit()
    return nc


_CACHE = {}
_DEBUG = False


def _host_consts():
    tri = np.triu(np.ones((128, 128), np.float32))
    negm = np.where(tri > 0, 0.0, NEG).astype(np.float32)
    m2 = np.zeros((128, 2, 256), np.float32)
    m2[:, 0, :128] = tri
    m2[:, 0, 128:] = 1.0
    m2[:, 1, 128:] = tri
    return dict(tri_f=tri, negm=negm, tri_b=tri.astype(ml_dtypes.bfloat16), mask2=m2.astype(ml_dtypes.bfloat16),
                ident_b=np.eye(128, dtype=np.float32).astype(ml_dtypes.bfloat16),
                ones_f=np.ones((128, 128), np.float32))


def kernel(x, pre_norm_w, post_norm_w, w_in, conv_w, conv_b, dt_bias, a_log, d_skip, ssm_norm_w,
           lambda_q1, lambda_k1, lambda_q2, lambda_k2, attn_subln_w, w_out):
    f32 = np.float32
    x = np.asarray(x, f32)
    perm = _perm_cols()
    w_in0 = np.asarray(w_in, f32)[0]
    w_in_p = np.ascontiguousarray(w_in0[:, perm])
    w_dt = np.ascontiguousarray(w_in0[:, 7168:7200])
    w_out0 = np.ascontiguousarray(np.asarray(w_out, f32)[0])
    rep = lambda v: np.ascontiguousarray(np.broadcast_to(np.asarray(v, f32).reshape(1, -1), (128, np.asarray(v).size)))
    ch = []
    for g in range(NG):
        ch += list(range(g * 512, (g + 1) * 512))
        ch += list(range(2048 + g * 128, 2048 + (g + 1) * 128))
        ch += list(range(2560 + g * 128, 2560 + (g + 1) * 128))
    ch = np.array(ch)
    cw = np.asarray(conv_w, f32)[0][:, ch]
    cw = np.ascontiguousarray(cw.reshape(4, 24, 128).transpose(2, 1, 0))
    cb = np.ascontiguousarray(np.asarray(conv_b, f32)[0][ch].reshape(24, 128).T)
    headvec = np.ascontiguousarray(np.broadcast_to(
        np.stack([np.asarray(dt_bias, f32)[0], np.asarray(a_log, f32)[0], np.asarray(d_skip, f32)[0]])[None], (128, 3, 32)))
    lam = np.ascontiguousarray(np.broadcast_to(
        np.stack([np.asarray(v, f32)[0] for v in (lambda_q1, lambda_k1, lambda_q2, lambda_k2)])[None], (128, 4, 64)))
    common = dict(
        w_in=w_in_p, w_dt=w_dt, w_out=w_out0,
        pre_w=np.ascontiguousarray(np.asarray(pre_norm_w, f32)[0].reshape(KC, 128).T),
        post_w=rep(np.asarray(post_norm_w)[0]), conv_w=cw, conv_b=cb, headvec=headvec,
        ssm_norm_w=rep(np.asarray(ssm_norm_w)[0]), lam=lam, subln_w=rep(np.asarray(attn_subln_w)[0]),
        subw_col=np.ascontiguousarray(np.asarray(attn_subln_w, f32)[0].reshape(128, 1)),
        **_host_consts())
    in_maps = []
    for c in range(8):
        b, j = c // 2, c % 2
        if j == 0:
            xl = np.concatenate([np.zeros((OWN, D), f32), x[b, :OWN]], axis=0)
        else:
            xl = x[b]
        m = dict(common)
        m["x"] = np.ascontiguousarray(xl)
        m["pmask"] = np.full((128, 1), float(j), f32)
        m["pbias"] = np.full((128, 1), 0.0 if j else NEG, f32)
        in_maps.append(m)
    if _DEBUG:
        return in_maps
    if "nc" not in _CACHE:
        _CACHE["nc"] = build_program()
    res = run_bass_kernel_spmd(_CACHE["nc"], in_maps, core_ids=list(range(8)))
    out = np.empty((NBATCH, SEQ, D), f32)
    for c in range(8):
        b, j = c // 2, c % 2
        out[b, j * OWN:(j + 1) * OWN] = res.results[c]["out"]
    return out
```

```python
import types
import numpy as np
import ml_dtypes
import concourse.bass as bass
import concourse.mybir as mybir
from concourse.bass_utils import run_bass_kernel_spmd

F32 = mybir.dt.float32
BF16 = mybir.dt.bfloat16
AF = mybir.ActivationFunctionType
ALU = mybir.AluOpType
AX = mybir.AxisListType

ENGS = ("pe", "act", "dve", "pool", "sp")
STRICT_SAME_ENGINE = True


def _freeze(fn):
    if fn is None or fn.__closure__ is None:
        return fn
    cells = []
    for c in fn.__closure__:
        try:
            cells.append(types.CellType(c.cell_contents))
        except ValueError:
            cells.append(c)
    return types.FunctionType(fn.__code__, fn.__globals__, fn.__name__, fn.__defaults__, tuple(cells))


class Op:
    __slots__ = ("eng", "fn", "deps", "dma", "signal", "count", "src", "name", "inc")


class Prog:
    def __init__(self, nc):
        self.nc = nc
        self.ops = {e: [] for e in ENGS}
        self.last_w = {}
        self.readers = {}
        self.dma_cnt = {}
        self.out_dmas = []
        self.last_op = {}
        self.frozen = False
        self.check = False
        self.stop = None

    def add(self, eng, fn, reads=(), writes=(), dma=None, ndma=1, name=None, out=False, inc=16):
        if self.frozen and not out:
            return None
        op = Op()
        op.eng, op.fn, op.dma, op.signal, op.count, op.name = eng, _freeze(fn), dma, False, None, name
        op.deps = {}
        op.inc = inc
        is_async = dma is not None

        def dep(d, raw):
            if d is None or d is op:
                return
            if d.dma is None and not is_async and d.eng == eng:
                if eng == "pe" or not (raw or STRICT_SAME_ENGINE):
                    return
            cur = op.deps.get(d.src)
            if cur is None or cur.count_key() < d.count_key():
                op.deps[d.src] = d

        for k in reads:
            dep(self.last_w.get(k), True)
        for k in writes:
            dep(self.last_w.get(k), False)
            for r in self.readers.get(k, {}).values():
                dep(r, False)
        if is_async:
            c = self.dma_cnt.get(dma, 0) + inc * ndma
            self.dma_cnt[dma] = c
            op.src = ("dma", dma)
            op.count = c
        else:
            op.src = eng
            op.count = len(self.ops[eng])
        for k in writes:
            self.last_w[k] = op
            self.readers[k] = {}
        for k in reads:
            self.readers.setdefault(k, {})[op.src] = op
        self.ops[eng].append(op)
        self.last_op[op.src] = op
        if out:
            self.out_dmas.append(op)
        return op

    def mark(self, name):
        if self.stop is not None and name == self.stop:
            self.frozen = True

    def barrier(self):
        lasts = dict(self.last_op)
        for e in ENGS:
            op = Op()
            op.eng, op.fn, op.dma, op.signal, op.count, op.name = e, None, None, False, None, "bar"
            op.inc = 16
            op.deps = {s: d for s, d in lasts.items() if s != e}
            op.src = e
            op.count = len(self.ops[e])
            self.ops[e].append(op)
        self.last_w = {}
        self.readers = {}

    def simulate(self):
        sem = {}
        pc = {e: 0 for e in ENGS}
        progress = True
        while progress:
            progress = False
            for e in ENGS:
                while pc[e] < len(self.ops[e]):
                    op = self.ops[e][pc[e]]
                    if any(sem.get(src, 0) < d.count for src, d in op.deps.items()):
                        break
                    if op.fn is not None:
                        if op.dma is not None:
                            sem[op.src] = sem.get(op.src, 0) + op.inc
                        elif op.signal:
                            sem[e] = sem.get(e, 0) + 1
                            assert sem[e] == op.count, (e, sem[e], op.count)
                    pc[e] += 1
                    progress = True
        stuck = {e: (pc[e], len(self.ops[e])) for e in ENGS if pc[e] < len(self.ops[e])}
        for e in stuck:
            op = self.ops[e][pc[e]]
            print("STUCK", e, pc[e], op.name, {src: (d.count, sem.get(src, 0)) for src, d in op.deps.items()})
        assert not stuck, stuck
        nw = {e: 0 for e in ENGS}
        for e in ENGS:
            seen = {}
            for op in self.ops[e]:
                for src, d in op.deps.items():
                    if seen.get(src, 0) < d.count:
                        seen[src] = d.count
                        nw[e] += 1
        print("simulate OK", {e: len(self.ops[e]) for e in ENGS}, "waits", nw)

    def emit(self):
        nc = self.nc
        fin = Op()
        fin.eng, fin.fn, fin.dma, fin.signal, fin.count, fin.name = "sp", None, None, False, None, "fin"
        fin.deps = {}
        for d in self.out_dmas:
            cur = fin.deps.get(d.src)
            if cur is None or cur.count < d.count:
                fin.deps[d.src] = d
        fin.src = "sp"
        self.ops["sp"].append(fin)
        for e in ENGS:
            for op in self.ops[e]:
                for d in op.deps.values():
                    d.signal = True
        for e in ENGS:
            c = 0
            for op in self.ops[e]:
                if op.dma is None:
                    if op.signal:
                        c += 1
                        op.count = c
                    else:
                        op.count = None
        srcs = list(ENGS) + [("dma", k) for k in self.dma_cnt]
        if self.check:
            self.simulate()
        from contextlib import ExitStack
        with ExitStack() as st:
            sems = {}
            for i, s in enumerate(srcs):
                sems[s] = st.enter_context(nc.semaphore(f"s{i}"))
            block = st.enter_context(nc.Block())
            engobj = {"pe": nc.tensor, "act": nc.scalar, "dve": nc.vector,
                      "pool": nc.gpsimd, "sp": nc.sync}

            def stream(e):
                def body(eng):
                    seen = {}
                    for op in self.ops[e]:
                        for src, d in op.deps.items():
                            if seen.get(src, 0) >= d.count:
                                continue
                            eng.wait_ge(sems[src], d.count)
                            seen[src] = d.count
                        if op.fn is None:
                            continue
                        r = op.fn(eng)
                        if op.dma is not None:
                            rl = r if isinstance(r, (list, tuple)) else [r]
                            for ins in rl:
                                ins.then_inc(sems[op.src], op.inc)
                        elif op.signal:
                            ins = r[-1] if isinstance(r, (list, tuple)) else r
                            ins.then_inc(sems[e], 1)
                return body

            block.tensor(stream("pe"))
            block.scalar(stream("act"))
            block.vector(stream("dve"))
            block.gpsimd(stream("pool"))
            block.sync(stream("sp"))


def _ck(self):
    return self.count


Op.count_key = _ck


D = 4096
SEQ = 4096
NBATCH = 4
OWN = 2048
TT = 512
NTILE = SEQ // TT
KC = D // 128
NG = 4
NH = 16
EPS = 1e-6
LAM_INIT = 0.8 - 0.6 * 1.0
NEG = -30000.0

PANELS = []
for _g in range(NG):
    PANELS.append((f"xs{_g}", _g * 1280, 512))
    PANELS.append((f"bc{_g}", _g * 1280 + 512, 256))
    PANELS.append((f"z{_g}", _g * 1280 + 768, 512))
for _i in range(4):
    PANELS.append((f"k{_i}", 5120 + _i * 512, 512))
for _i in range(4):
    PANELS.append((f"v{_i}", 7168 + _i * 512, 512))
for _i in range(4):
    PANELS.append((f"q{_i}", 9216 + _i * 512, 512))
for _i in range(4):
    PANELS.append((f"za{_i}", 11264 + _i * 512, 512))
NCOL = 13312


def _perm_cols():
    idx = []
    for g in range(NG):
        idx += list(range(4096 + g * 512, 4096 + (g + 1) * 512))
        idx += list(range(4096 + 2048 + g * 128, 4096 + 2048 + (g + 1) * 128))
        idx += list(range(4096 + 2560 + g * 128, 4096 + 2560 + (g + 1) * 128))
        idx += list(range(g * 512, (g + 1) * 512))
    idx += list(range(9248, 11296))
    idx += list(range(11296, 13344))
    idx += list(range(7200, 9248))
    idx += list(range(2048, 4096))
    return np.array(idx, dtype=np.int64)


def build_program(ntiles=NTILE, do_attn=True, do_out=True, dump=False, ncast=3, stop=None):
    nc = bass.Bass("TRN2", target_bir_lowering=False)
    P = Prog(nc)
    P.stop = stop
    A = P.add

    def din(name, shape, dt=F32):
        return nc.dram_tensor(name, list(shape), dt, kind="ExternalInput").ap()

    def dscr(name, shape, dt=BF16):
        return nc.dram_tensor(name, list(shape), dt).ap()

    x_d = din("x", [SEQ, D])
    win_d = din("w_in", [D, NCOL])
    wdt_d = din("w_dt", [D, 32])
    wout_d = din("w_out", [D, D])
    prew_d = din("pre_w", [128, KC])
    postw_d = din("post_w", [128, D])
    convw_d = din("conv_w", [128, 24, 4])
    convb_d = din("conv_b", [128, 24])
    hv_d = din("headvec", [128, 3, 32])
    snw_d = din("ssm_norm_w", [128, 2048])
    lam_d = din("lam", [128, 4, 64])
    subw_d = din("subln_w", [128, 128])
    pmask_d = din("pmask", [128, 1])
    pbias_d = din("pbias", [128, 1])
    subc_d = din("subw_col", [128, 1])
    trif_d = din("tri_f", [128, 128])
    negm_d = din("negm", [128, 128])
    trib_d = din("tri_b", [128, 128], BF16)
    mask2_d = din("mask2", [128, 2, 256], BF16)
    identb_d = din("ident_b", [128, 128], BF16)
    onesf_d = din("ones_f", [128, 128])
    out_d = nc.dram_tensor("out", [OWN, D], F32, kind="ExternalOutput").ap()

    wbp = {n: dscr("wb_" + n, [128, KC, w]) for (n, _, w) in PANELS}
    wbdt = dscr("wb_dt", [128, KC, 32])
    wbo = [dscr(f"wb_o{i}", [128, KC, 512]) for i in range(8)]
    kT_d = dscr("kT_d", [NH, 128, SEQ])
    qT_d = dscr("qT_d", [NH, 128, OWN])
    v_d = dscr("v_d", [SEQ, NH * 128])
    zaT_d = dscr("zaT_d", [NH, 128, OWN])
    yT_d = dscr("yT_d", [D, OWN])

    dbg_out = {}

    def ddump(name, shape, dt=F32):
        t = nc.dram_tensor("dbg_" + name, list(shape), dt, kind="ExternalOutput").ap()
        dbg_out[name] = t
        return t

    SB_LO, SB_HI = 16640, 229376
    cur = [SB_LO]
    cnt = [0]

    def sb(shape, dt, name=None):
        nbytes = int(np.prod(shape[1:])) * (2 if dt == BF16 else 4)
        nbytes = (nbytes + 63) // 64 * 64
        off = cur[0]
        cur[0] += nbytes
        assert cur[0] <= SB_HI, (name, cur[0])
        cnt[0] += 1
        return nc.alloc_sbuf_tensor_at(f"{name or 't'}_{cnt[0]}", list(shape), dt, offset=off)

    pcnt = [0]

    def psum(shape, dt=F32):
        pcnt[0] += 1
        return nc.alloc_psum_tensor(f"ps{pcnt[0]}", list(shape), dt)

    prew = sb([128, KC], F32, "prew")
    convw = sb([128, 24, 4], F32, "convw")
    convb = sb([128, 24], F32, "convb")
    hv = sb([128, 3, 32], F32, "hv")
    lam_sb = sb([128, 4, 64], F32, "lam")
    subw = sb([128, 128], F32, "subw")
    pmask = sb([128, 1], F32, "pmask")
    pbias = sb([128, 1], F32, "pbias")
    subc = sb([128, 1], F32, "subc")
    trif = sb([128, 128], F32, "trif")
    negm = sb([128, 128], F32, "negm")
    trib = sb([128, 128], BF16, "trib")
    mask2 = sb([128, 2, 256], BF16, "mask2")
    identb = sb([128, 128], BF16, "identb")
    onesf = sb([128, 128], F32, "onesf")
    A_b = sb([128, 32], F32, "A_b")
    neglam = sb([128, 1], F32, "neglam")
    lamtmp = sb([128, 4], F32, "lamtmp")
    lamjunk = sb([128, 64], F32, "lamjunk")
    cvec = sb([128, 4], F32, "cvec")
    subw8 = sb([128, 128], F32, "subw8")
    EPSB = cvec[:, 0:1]
    ONEB = cvec[:, 1:2]
    A("pool", lambda e: e.memset(cvec[:, 0:1], EPS), writes=["cvec"])
    A("pool", lambda e: e.memset(cvec[:, 1:2], 1.0), writes=["cvec"])

    consts = [(prew, prew_d), (convw, convw_d), (convb, convb_d), (hv, hv_d), (lam_sb, lam_d),
              (subw, subw_d), (pmask, pmask_d), (pbias, pbias_d), (subc, subc_d), (trif, trif_d), (negm, negm_d), (trib, trib_d), (mask2, mask2_d),
              (identb, identb_d), (onesf, onesf_d)]
    for t, d in consts:
        A("sp", (lambda t=t, d=d: lambda e: e.dma_start(out=t[:], in_=d[:]))(), writes=["consts"], dma="consts")

    A("dve", lambda e: e.tensor_scalar(out=subc[:, :], in0=subc[:, :], scalar1=1.0 - LAM_INIT, scalar2=None, op0=ALU.mult),
      reads=["consts"], writes=["subc8"])
    A("act", lambda e: e.activation(out=A_b[:, :], in_=hv[:, 1, :], func=AF.Exp), reads=["consts"], writes=["A_b0"])
    A("dve", lambda e: e.tensor_scalar(out=A_b[:, :], in0=A_b[:, :], scalar1=-1.0, scalar2=None, op0=ALU.mult),
      reads=["A_b0"], writes=["A_b"])
    for i in range(2):
        A("dve", (lambda i=i: lambda e: e.scalar_tensor_tensor(
            out=lamjunk[:, :], in0=lam_sb[:, 2 * i, :], scalar=1.0, in1=lam_sb[:, 2 * i + 1, :],
            op0=ALU.mult, op1=ALU.mult, accum_out=lamtmp[:, i:i + 1]))(),
          reads=["consts"], writes=["lamjunk", f"lamt{i}"])
    A("act", lambda e: e.activation(out=lamtmp[:, 2:4], in_=lamtmp[:, 0:2], func=AF.Exp),
      reads=["lamt0", "lamt1"], writes=["lame"])
    A("dve", lambda e: e.tensor_tensor(out=neglam[:, :], in0=lamtmp[:, 3:4], in1=lamtmp[:, 2:3], op=ALU.subtract),
      reads=["lame"], writes=["neglam0"])
    A("dve", lambda e: e.tensor_scalar(out=neglam[:, :], in0=neglam[:, :], scalar1=-LAM_INIT, scalar2=None, op0=ALU.add),
      reads=["neglam0"], writes=["neglam"])

    cast_done = set()
    cast_i = [0]

    def cast(name):
        if name in cast_done:
            return
        cast_done.add(name)
        key = f"cast{cast_i[0] % ncast}"
        cast_i[0] += 1
        if name == "dt":
            src, dst = wdt_d[:, :], wbdt
        elif name.startswith("o"):
            j = int(name[1:])
            src, dst = wout_d[:, j * 512:(j + 1) * 512], wbo[j]
        else:
            off, w = [(o, w) for (n, o, w) in PANELS if n == name][0]
            src, dst = win_d[:, off:off + w], wbp[name]
        A("pool", lambda e: e.dma_start(out=dst.rearrange("p kc n -> kc p n"),
                                        in_=src.rearrange("(kc p) n -> kc p n", p=128)),
          reads=[key], writes=[key, "wb_" + name], dma=key)

    P1 = cur[0]
    xins = [sb([128, D], F32, f"xin{i}") for i in range(2)]
    xn = sb([128, D], BF16, "xn")
    hT = sb([128, KC, TT], BF16, "hT")
    wpan = [sb([128, KC, 512], BF16, f"wpan{i}") for i in range(2)]
    wdt = sb([128, KC, 32], BF16, "wdt")
    stat = sb([128, 8], F32, "stat")
    statq = sb([128, 8], F32, "statq")
    halo = sb([128, 24, 3], F32, "halo")
    utmp = [sb([128, TT + 3], F32, f"utmp{i}") for i in range(2)]
    cacc = [sb([128, TT], F32, f"cacc{i}") for i in range(2)]
    fm = sb([128, 6, TT], BF16, "fm")
    xs_tok = sb([128, 4, 512], BF16, "xs_tok")
    b_tok = sb([128, 4, 128], BF16, "b_tok")
    sz = sb([128, 4, 512], BF16, "sz")
    y_tok = sb([128, 4, 512], BF16, "y_tok")
    dt_sb = sb([128, 4, 32], F32, "dt_sb")
    a_sb = sb([128, 4, 32], F32, "a_sb")
    cs_sb = sb([128, 32], F32, "cs_sb")
    ecs = sb([128, 32], F32, "ecs")
    dsd = sb([128, 32], F32, "dsd")
    dec = sb([128, 32], F32, "dec")
    dmat = sb([128, 4, 128], F32, "dmat")
    Lt = sb([128, 4, 128], BF16, "Lt")
    cbt = sb([128, 128], BF16, "cbt")
    Gm = sb([128, 8, 128], BF16, "Gm")
    Xb = sb([128, 512], BF16, "Xb")
    Xd = sb([128, 512], BF16, "Xd")
    xsD = sb([128, 512], BF16, "xsD")
    t1 = sb([128, 512], F32, "t1")
    t2 = sb([128, 512], F32, "t2")
    junkf = sb([128, 512], BF16, "junkf")
    Sin = sb([128, NG, 512], F32, "Sin")
    Sin_bf = sb([128, NG, 512], BF16, "Sin_bf")
    snw = sb([128, 512], F32, "snw")
    stg = [sb([128, 512], BF16, f"stg{i}") for i in range(2)]
    yTs = sb([128, 4, TT], BF16, "yTs")

    tr_f32 = [psum([128, 512]) for _ in range(2)]
    ps_trs = [t[:, :].bitcast(BF16) for t in tr_f32]
    pbig = [psum([128, 1024]) for _ in range(3)]
    banks = [pbig[i // 2][:, (i % 2) * 512:(i % 2 + 1) * 512] for i in range(6)]
    ps_mm = banks[0:2]
    ps_R = banks[2][:, :].rearrange("p (r l) -> p r l", r=4)
    ps_misc = banks[3]
    ps_y = banks[4]
    ps_yo = banks[5]
    ps_sc = banks[5]
    cstot = sb([128, 40], F32, "cstot")

    A("pool", lambda e: e.memset(halo[:, :, :], 0.0), writes=["halo"])
    A("pool", lambda e: e.memset(Sin[:, :, :], 0.0), writes=["Sin"])
    A("pool", lambda e: e.memset(Sin_bf[:, :, :], 0.0), writes=["Sin_bf"])

    cast("dt")
    A("sp", lambda e: e.dma_start(out=wdt[:, :, :], in_=wbdt[:, :, :]), reads=["wb_dt"], writes=["wdt"], dma="wdt")

    tile_panels_prefix = ["xs", "bc"], ["k", "v"]
    pan_i = [0]
    cur_tile = [0]
    stg_i = [0]
    tr_i = [0]

    later = [f"z{i}" for i in range(4)] + [f"q{i}" for i in range(4)] + [f"za{i}" for i in range(4)] + [f"o{i}" for i in range(8)]

    def load_panel(name):
        slot = pan_i[0] % 2
        pan_i[0] += 1
        w = [w for (n, _, w) in PANELS if n == name][0]
        cast(name)
        if cur_tile[0] >= 1 and later and pan_i[0] % 2 == 0:
            cast(later.pop(0))
        A("sp", lambda e: e.dma_start(out=wpan[slot][:, :, 0:w], in_=wbp[name][:, :, :]),
          reads=["wb_" + name], writes=[f"wpan{slot}"], dma=f"wpan{slot}")
        return slot, w

    def evac_store(src_ap, dst_ap, eng, scale=None, func=None):
        si = stg_i[0] % 2
        stg_i[0] += 1
        st = stg[si]
        if eng == "act":
            A("act", lambda e: e.activation(out=st[:, :], in_=src_ap, func=func or AF.Copy,
                                            scale=1.0 if scale is None else scale),
              reads=[src_key[0]], writes=[f"stg{si}"])
        else:
            A("dve", lambda e: e.tensor_copy(out=st[:, :], in_=src_ap), reads=[src_key[0]], writes=[f"stg{si}"])
        A("pool", lambda e: e.dma_start(out=dst_ap, in_=st[:, :]), reads=[f"stg{si}"], dma=f"stg{si}")

    src_key = [None]
    mm_i = [0]

    def transposes4(srcs, evac):
        h = tr_i[0] % 2
        tr_i[0] += 1
        for j, (ap, rk) in enumerate(srcs):
            A("pe", (lambda ap=ap, j=j: lambda e: e.transpose(ps_trs[h][:, j * 128:(j + 1) * 128], ap, identb[:, :]))(),
              reads=rk + ["consts"], writes=[f"ps_tr{h}"])
        evac(ps_trs[h][:, 0:512], f"ps_tr{h}")

    for tile in range(ntiles):
        own = tile >= 4
        t0 = tile * TT
        cur_tile[0] = tile
        for s in range(4):
            r0 = t0 + s * 128
            xi = (tile * 4 + s) % 2
            xin = xins[xi]
            xk = f"xin{xi}"
            if tile == 0 and s == 0:
                A("sp", lambda e: e.dma_start(out=xin[:, :], in_=x_d[r0:r0 + 128, :]), writes=[xk], dma=xk)
            nxt = tile * 4 + s + 1
            if nxt < ntiles * 4:
                xo = xins[nxt % 2]
                rn = nxt * 128
                A("sp", lambda e: e.dma_start(out=xo[:, :], in_=x_d[rn:rn + 128, :]), writes=[f"xin{nxt % 2}"], dma=f"xin{nxt % 2}")
            for c8 in range(8):
                A("act", lambda e: e.activation(out=junkf[:, :], in_=xin[:, c8 * 512:(c8 + 1) * 512], func=AF.Square,
                                                accum_out=statq[:, c8:c8 + 1]),
                  reads=[xk], writes=["junkf", "statq"])
            A("dve", lambda e: e.tensor_reduce(out=stat[:, 0:1], in_=statq[:, :], axis=AX.X, op=ALU.add), reads=["statq"], writes=["stat0"])
            A("act", lambda e: e.activation(out=stat[:, 1:2], in_=stat[:, 0:1], func=AF.Sqrt, scale=1.0 / D, bias=EPSB[:, :]),
              reads=["stat0", "cvec"], writes=["stat1"])
            P.mark("n_sqrt")
            A("dve", lambda e: e.reciprocal(out=stat[:, 2:3], in_=stat[:, 1:2]), reads=["stat1"], writes=["stat2"])
            A("dve", lambda e: e.tensor_scalar(out=xn[:, :], in0=xin[:, :], scalar1=stat[:, 2:3], scalar2=None, op0=ALU.mult),
              reads=[xk, "stat2"], writes=["xn"])
            P.mark("n_scale")
            for q4 in range(8):
                def ev(ps_ap, key, q4=q4, s=s):
                    eng = "dve" if q4 % 2 == 0 else "pool"
                    eng = "dve"
                    A(eng, lambda e: e.tensor_tensor(
                        out=hT[:, q4 * 4:(q4 + 1) * 4, s * 128:(s + 1) * 128],
                        in0=ps_ap.rearrange("p (j t) -> p j t", j=4),
                        in1=prew[:, q4 * 4:(q4 + 1) * 4].unsqueeze(2).to_broadcast([128, 4, 128]), op=ALU.mult),
                      reads=[key, "consts"], writes=["hT"])
                transposes4([(xn[:, (q4 * 4 + j) * 128:(q4 * 4 + j + 1) * 128], ["xn"]) for j in range(4)], ev)
                P.mark(f"n_tr{s}_{q4}")

        P.mark(f"norm{tile}")
        for s in range(4):
            for kc in range(KC):
                A("pe", (lambda kc=kc, s=s: lambda e: e.matmul(ps_misc[:, 192:224], lhsT=hT[:, kc, s * 128:(s + 1) * 128],
                                                            rhs=wdt[:, kc, :], start=(kc == 0), stop=(kc == KC - 1)))(),
                  reads=["hT", "wdt"], writes=["ps_misc"])
            A("dve", (lambda s=s: lambda e: e.tensor_tensor(out=dt_sb[:, s, :], in0=ps_misc[:, 192:224], in1=hv[:, 0, :], op=ALU.add))(),
              reads=["ps_misc", "consts"], writes=["dt_sb"])
        A("act", lambda e: e.activation(out=dt_sb[:, :, :], in_=dt_sb[:, :, :], func=AF.Exp), reads=["dt_sb"], writes=["dt_sb"])
        A("act", lambda e: e.activation(out=dt_sb[:, :, :], in_=dt_sb[:, :, :], func=AF.Ln, bias=ONEB[:, :]), reads=["dt_sb", "cvec"], writes=["dt_sb"])
        A("dve", lambda e: e.tensor_tensor(out=a_sb[:, :, :], in0=dt_sb[:, :, :],
                                           in1=A_b[:, :].unsqueeze(1).to_broadcast([128, 4, 32]), op=ALU.mult),
          reads=["dt_sb", "A_b"], writes=["a_sb"])

        P.mark(f"dt{tile}")
        for g in range(NG):
            for pname, nchunk, c0 in ((f"xs{g}", 4, 0), (f"bc{g}", 2, 4)):
                slot, w = load_panel(pname)
                for cc in range(nchunk):
                    bank = mm_i[0] % 2
                    mm_i[0] += 1
                    for kc in range(KC):
                        A("pe", (lambda kc=kc, cc=cc, slot=slot, bank=bank: lambda e: e.matmul(
                            ps_mm[bank][:, :], lhsT=wpan[slot][:, kc, cc * 128:(cc + 1) * 128], rhs=hT[:, kc, :],
                            start=(kc == 0), stop=(kc == KC - 1)))(),
                          reads=[f"wpan{slot}", "hT"], writes=[f"ps_mm{bank}"])
                    ch = g * 6 + c0 + cc
                    u = utmp[cc % 2]
                    ca = cacc[cc % 2]
                    uk, ck = f"utmp{cc % 2}", f"cacc{cc % 2}"
                    A("act", (lambda u=u, bank=bank: lambda e: e.copy(out=u[:, 3:TT + 3], in_=ps_mm[bank][:, :]))(),
                      reads=[f"ps_mm{bank}"], writes=[uk])
                    A("pool", (lambda u=u, ch=ch: lambda e: e.tensor_copy(out=u[:, 0:3], in_=halo[:, ch, :]))(),
                      reads=["halo"], writes=[uk + "h"])
                    A("dve", (lambda u=u, ca=ca, ch=ch: lambda e: e.tensor_scalar(
                        out=ca[:, :], in0=u[:, 0:TT], scalar1=convw[:, ch, 0:1], scalar2=None, op0=ALU.mult))(),
                      reads=[uk, uk + "h", "consts"], writes=[ck])
                    for j in range(1, 4):
                        A("dve", (lambda u=u, ca=ca, ch=ch, j=j: lambda e: e.scalar_tensor_tensor(
                            out=ca[:, :], in0=u[:, j:j + TT], scalar=convw[:, ch, j:j + 1], in1=ca[:, :],
                            op0=ALU.mult, op1=ALU.add))(),
                          reads=[uk, uk + "h", ck, "consts"], writes=[ck])
                    A("pool", (lambda u=u, ch=ch: lambda e: e.tensor_copy(out=halo[:, ch, :], in_=u[:, TT:TT + 3]))(),
                      reads=[uk], writes=["halo"])
                    A("act", (lambda ca=ca, ch=ch, c0=c0, cc=cc: lambda e: e.activation(
                        out=fm[:, c0 + cc, :], in_=ca[:, :], func=AF.Silu, bias=convb[:, ch:ch + 1]))(),
                      reads=[ck, "consts"], writes=[f"fm{c0 + cc}"])
            P.mark(f"conv{tile}_{g}")
            for s in range(4):
                def ev(ps_ap, key, s=s):
                    A("act", lambda e: e.copy(out=xs_tok[:, s, :], in_=ps_ap), reads=[key], writes=["xs_tok"])
                transposes4([(fm[:, c, s * 128:(s + 1) * 128], [f"fm{c}"]) for c in range(4)], ev)

            def evb(ps_ap, key):
                A("act", lambda e: e.copy(out=b_tok[:, :, :], in_=ps_ap.rearrange("p (s n) -> p s n", s=4)),
                  reads=[key], writes=["b_tok"])
            transposes4([(fm[:, 4, s * 128:(s + 1) * 128], ["fm4"]) for s in range(4)], evb)

            P.mark(f"tok{tile}_{g}")
            if own:
                A("sp", lambda e: e.dma_start(out=snw[:, :], in_=snw_d[:, g * 512:(g + 1) * 512]), writes=["snw"], dma="snw")
                slot, w = load_panel(f"z{g}")
                for s in range(4):
                    bank = mm_i[0] % 2
                    mm_i[0] += 1
                    for kc in range(KC):
                        A("pe", (lambda kc=kc, s=s, slot=slot, bank=bank: lambda e: e.matmul(
                            ps_mm[bank][:, :], lhsT=hT[:, kc, s * 128:(s + 1) * 128], rhs=wpan[slot][:, kc, :],
                            start=(kc == 0), stop=(kc == KC - 1)))(),
                          reads=[f"wpan{slot}", "hT"], writes=[f"ps_mm{bank}"])
                    A("act", (lambda s=s, bank=bank: lambda e: e.activation(out=sz[:, s, :], in_=ps_mm[bank][:, :], func=AF.Silu))(),
                      reads=[f"ps_mm{bank}"], writes=["sz"])

            P.mark(f"z{tile}_{g}")
            h0 = g * 8
            for s in range(4):
                A("pe", (lambda s=s: lambda e: e.matmul(ps_misc[:, 128:136], lhsT=trif[:, :], rhs=a_sb[:, s, h0:h0 + 8], start=True, stop=True))(),
                  reads=["a_sb", "consts"], writes=["ps_misc"])
                A("pe", (lambda s=s: lambda e: e.matmul(ps_misc[:, 136:144], lhsT=onesf[:, :], rhs=a_sb[:, s, h0:h0 + 8], start=True, stop=True))(),
                  reads=["a_sb", "consts"], writes=["ps_misc"])
                A("dve", lambda e: e.tensor_copy(out=cstot[:, 0:16], in_=ps_misc[:, 128:144]), reads=["ps_misc"], writes=["cs_sb"])
                A("act", lambda e: e.activation(out=dec[:, 0:8], in_=cstot[:, 8:16], func=AF.Exp), reads=["cs_sb"], writes=["dec"])
                A("dve", lambda e: e.tensor_tensor(out=dsd[:, 0:8], in0=cstot[:, 8:16], in1=cstot[:, 0:8], op=ALU.subtract),
                  reads=["cs_sb"], writes=["dsd0"])
                A("act", lambda e: e.activation(out=dsd[:, 0:8], in_=dsd[:, 0:8], func=AF.Exp), reads=["dsd0"], writes=["dsd"])
                A("dve", (lambda s=s: lambda e: e.tensor_tensor(
                    out=Xb[:, :].rearrange("p (r d) -> p r d", r=8), in0=xs_tok[:, s, :].rearrange("p (r d) -> p r d", r=8),
                    in1=dt_sb[:, s, h0:h0 + 8].unsqueeze(2).to_broadcast([128, 8, 64]), op=ALU.mult))(),
                  reads=["xs_tok", "dt_sb"], writes=["Xb"])
                A("dve", lambda e: e.tensor_tensor(
                    out=Xd[:, :].rearrange("p (r d) -> p r d", r=8), in0=Xb[:, :].rearrange("p (r d) -> p r d", r=8),
                    in1=dsd[:, 0:8].unsqueeze(2).to_broadcast([128, 8, 64]), op=ALU.mult),
                  reads=["Xb", "dsd"], writes=["Xd"])
                if own:
                    A("act", lambda e: e.activation(out=ecs[:, 0:8], in_=cstot[:, 0:8], func=AF.Exp), reads=["cs_sb"], writes=["ecs"])
                    A("pe", (lambda s=s: lambda e: e.matmul(ps_misc[:, 0:128], lhsT=fm[:, 4, s * 128:(s + 1) * 128],
                                                         rhs=fm[:, 5, s * 128:(s + 1) * 128], start=True, stop=True))(),
                      reads=["fm4", "fm5"], writes=["ps_misc"])
                    A("dve", lambda e: e.tensor_tensor(out=cbt[:, :], in0=ps_misc[:, 0:128], in1=trib[:, :], op=ALU.mult),
                      reads=["ps_misc", "consts"], writes=["cbt"])
                    A("pool", (lambda s=s: lambda e: e.tensor_tensor(
                        out=xsD[:, :].rearrange("p (r d) -> p r d", r=8), in0=xs_tok[:, s, :].rearrange("p (r d) -> p r d", r=8),
                        in1=hv[:, 2, h0:h0 + 8].unsqueeze(2).to_broadcast([128, 8, 64]), op=ALU.mult))(),
                      reads=["xs_tok", "consts"], writes=["xsD"])
                    for hh in range(2):
                        for r in range(4):
                            rr = hh * 4 + r
                            A("pe", (lambda s=s, r=r, rr=rr: lambda e: e.matmul(
                                ps_R[:, r, :], lhsT=a_sb[:, s, h0 + rr:h0 + rr + 1].to_broadcast([128, 128]), rhs=trif[:, :],
                                start=True, stop=True))(),
                              reads=["a_sb", "consts"], writes=["ps_R"])
                        for r in range(4):
                            rr = hh * 4 + r
                            A("dve", (lambda r=r, rr=rr: lambda e: e.scalar_tensor_tensor(
                                out=dmat[:, r, :], in0=ps_R[:, r, :], scalar=cstot[:, rr:rr + 1], in1=negm[:, :],
                                op0=ALU.subtract, op1=ALU.add))(),
                              reads=["ps_R", "cs_sb", "consts"], writes=["dmat"])
                        A("act", lambda e: e.activation(out=Lt[:, :, :], in_=dmat[:, :, :], func=AF.Exp), reads=["dmat"], writes=["Lt"])
                        A("dve", (lambda hh=hh: lambda e: e.tensor_tensor(
                            out=Gm[:, hh * 4:(hh + 1) * 4, :], in0=Lt[:, :, :],
                            in1=cbt[:, :].unsqueeze(1).to_broadcast([128, 4, 128]), op=ALU.mult))(),
                          reads=["Lt", "cbt"], writes=["Gm"])
                    A("pe", lambda e: e.matmul(ps_y[:, :], lhsT=identb[:, :], rhs=xsD[:, :], start=True, stop=False),
                      reads=["xsD", "consts"], writes=["ps_y"])
                    for rr in range(8):
                        A("pe", (lambda rr=rr: lambda e: e.matmul(
                            ps_y[:, rr * 64:(rr + 1) * 64], lhsT=Gm[:, rr, :], rhs=Xb[:, rr * 64:(rr + 1) * 64],
                            start=False, stop=(rr == 7)))(),
                          reads=["Gm", "Xb"], writes=["ps_y"])
                    A("pe", (lambda s=s: lambda e: e.matmul(ps_yo[:, :], lhsT=fm[:, 5, s * 128:(s + 1) * 128],
                                                         rhs=Sin_bf[:, g, :], start=True, stop=True))(),
                      reads=["fm5", "Sin_bf"], writes=["ps_ys"])
                    A("dve", lambda e: e.tensor_tensor(
                        out=t1[:, :].rearrange("p (r d) -> p r d", r=8), in0=ps_yo[:, :].rearrange("p (r d) -> p r d", r=8),
                        in1=ecs[:, 0:8].unsqueeze(2).to_broadcast([128, 8, 64]), op=ALU.mult),
                      reads=["ps_ys", "ecs"], writes=["t1"])
                    A("dve", lambda e: e.tensor_tensor(out=t2[:, :], in0=ps_y[:, :], in1=t1[:, :], op=ALU.add),
                      reads=["ps_y", "t1"], writes=["t2"])
                    A("dve", (lambda s=s: lambda e: e.tensor_tensor(out=t1[:, :], in0=t2[:, :], in1=sz[:, s, :], op=ALU.mult))(),
                      reads=["t2", "sz"], writes=["t1"])
                    A("act", lambda e: e.activation(out=junkf[:, :], in_=t1[:, :], func=AF.Square, accum_out=stat[:, 3:4]),
                      reads=["t1"], writes=["junkf", "stat3"])
                    A("act", lambda e: e.activation(out=stat[:, 4:5], in_=stat[:, 3:4], func=AF.Sqrt, scale=1.0 / 512, bias=EPSB[:, :]),
                      reads=["stat3", "cvec"], writes=["stat4"])
                    A("dve", lambda e: e.reciprocal(out=stat[:, 5:6], in_=stat[:, 4:5]), reads=["stat4"], writes=["stat5"])
                    A("dve", (lambda s=s: lambda e: e.scalar_tensor_tensor(
                        out=y_tok[:, s, :], in0=t1[:, :], scalar=stat[:, 5:6], in1=snw[:, :],
                        op0=ALU.mult, op1=ALU.mult))(),
                      reads=["t1", "stat5", "snw"], writes=["y_tok"])
                A("pe", (lambda s=s: lambda e: e.matmul(ps_sc[:, :], lhsT=b_tok[:, s, :], rhs=Xd[:, :], start=True, stop=True))(),
                  reads=["b_tok", "Xd"], writes=["ps_ys"])
                A("dve", lambda e: e.tensor_tensor(
                    out=Sin[:, g, :].rearrange("p (r d) -> p r d", r=8), in0=Sin[:, g, :].rearrange("p (r d) -> p r d", r=8),
                    in1=dec[:, 0:8].unsqueeze(2).to_broadcast([128, 8, 64]), op=ALU.mult),
                  reads=["Sin", "dec"], writes=["Sin"])
                A("dve", lambda e: e.tensor_tensor(out=Sin[:, g, :], in0=Sin[:, g, :], in1=ps_sc[:, :], op=ALU.add),
                  reads=["Sin", "ps_ys"], writes=["Sin"])
                if tile == 3 and s == 3:
                    A("dve", lambda e: e.tensor_scalar(out=Sin[:, g, :], in0=Sin[:, g, :], scalar1=pmask[:, 0:1], scalar2=None, op0=ALU.mult),
                      reads=["Sin", "consts"], writes=["Sin"])
                if own or (tile == 3 and s == 3):
                    A("act", lambda e: e.copy(out=Sin_bf[:, g, :], in_=Sin[:, g, :]), reads=["Sin"], writes=["Sin_bf"])
            P.mark(f"ssd{tile}_{g}")
            if own:
                for c in range(4):
                    def ev(ps_ap, key, c=c):
                        A("act", lambda e: e.copy(out=yTs[:, c, :], in_=ps_ap), reads=[key], writes=["yTs"])
                    transposes4([(y_tok[:, s, c * 128:(c + 1) * 128], ["y_tok"]) for s in range(4)], ev)
                o0 = (tile - 4) * TT
                A("pool", (lambda g=g, o0=o0: lambda e: e.dma_start(
                    out=yT_d[g * 512:(g + 1) * 512, o0:o0 + TT].rearrange("(c p) t -> p c t", p=128), in_=yTs[:, :, :]))(),
                  reads=["yTs"], dma="yTs")

        P.mark(f"groups{tile}")
        for i in range(4):
            slot, w = load_panel(f"k{i}")
            for cc in range(4):
                bank = mm_i[0] % 2
                mm_i[0] += 1
                for kc in range(KC):
                    A("pe", (lambda kc=kc, cc=cc, slot=slot, bank=bank: lambda e: e.matmul(
                        ps_mm[bank][:, :], lhsT=wpan[slot][:, kc, cc * 128:(cc + 1) * 128], rhs=hT[:, kc, :],
                        start=(kc == 0), stop=(kc == KC - 1)))(),
                      reads=[f"wpan{slot}", "hT"], writes=[f"ps_mm{bank}"])
                src_key[0] = f"ps_mm{bank}"
                evac_store(ps_mm[bank][:, :], kT_d[i * 4 + cc, :, t0:t0 + TT], "dve" if cc % 2 else "act")
        for i in range(4):
            slot, w = load_panel(f"v{i}")
            for s in range(4):
                bank = mm_i[0] % 2
                mm_i[0] += 1
                for kc in range(KC):
                    A("pe", (lambda kc=kc, s=s, slot=slot, bank=bank: lambda e: e.matmul(
                        ps_mm[bank][:, :], lhsT=hT[:, kc, s * 128:(s + 1) * 128], rhs=wpan[slot][:, kc, :],
                        start=(kc == 0), stop=(kc == KC - 1)))(),
                      reads=[f"wpan{slot}", "hT"], writes=[f"ps_mm{bank}"])
                src_key[0] = f"ps_mm{bank}"
                evac_store(ps_mm[bank][:, :], v_d[t0 + s * 128:t0 + (s + 1) * 128, i * 512:(i + 1) * 512], "dve" if s % 2 else "act")
        if own:
            o0 = (tile - 4) * TT
            for i in range(4):
                slot, w = load_panel(f"q{i}")
                for cc in range(4):
                    bank = mm_i[0] % 2
                    mm_i[0] += 1
                    for kc in range(KC):
                        A("pe", (lambda kc=kc, cc=cc, slot=slot, bank=bank: lambda e: e.matmul(
                            ps_mm[bank][:, :], lhsT=wpan[slot][:, kc, cc * 128:(cc + 1) * 128], rhs=hT[:, kc, :],
                            start=(kc == 0), stop=(kc == KC - 1)))(),
                          reads=[f"wpan{slot}", "hT"], writes=[f"ps_mm{bank}"])
                    src_key[0] = f"ps_mm{bank}"
                    evac_store(ps_mm[bank][:, :], qT_d[i * 4 + cc, :, o0:o0 + TT], "act", scale=0.125)
            for i in range(4):
                slot, w = load_panel(f"za{i}")
                for cc in range(4):
                    bank = mm_i[0] % 2
                    mm_i[0] += 1
                    for kc in range(KC):
                        A("pe", (lambda kc=kc, cc=cc, slot=slot, bank=bank: lambda e: e.matmul(
                            ps_mm[bank][:, :], lhsT=wpan[slot][:, kc, cc * 128:(cc + 1) * 128], rhs=hT[:, kc, :],
                            start=(kc == 0), stop=(kc == KC - 1)))(),
                          reads=[f"wpan{slot}", "hT"], writes=[f"ps_mm{bank}"])
                    src_key[0] = f"ps_mm{bank}"
                    evac_store(ps_mm[bank][:, :], zaT_d[i * 4 + cc, :, o0:o0 + TT], "act", func=AF.Silu)

    if dump:
        for nm, t, key in (("hT", hT, "hT"), ("xn", xn, "xn"), ("stat", stat, "stat2"), ("fm", fm, "fm0"), ("dt", dt_sb, "dt_sb"), ("a", a_sb, "a_sb"), ("xs_tok", xs_tok, "xs_tok"),
                           ("b_tok", b_tok, "b_tok"), ("Sin", Sin, "Sin"), ("y_tok", y_tok, "y_tok"), ("sz", sz, "sz"), ("Gm", Gm, "Gm"),
                           ("cs", cstot, "cs_sb"), ("t2", t2, "t2"), ("Xb", Xb, "Xb"), ("Xd", Xd, "Xd"), ("cbt", cbt, "cbt")):
            dd = ddump(nm, list(t.shape), t.dtype)
            A("pool", (lambda dd=dd, t=t: lambda e: e.dma_start(out=dd[:], in_=t[:]))(), reads=[key], dma="dump_" + nm, out=True)
    if not do_attn:
        P.emit()
        return nc
    P.barrier()
    cur[0] = P1
    for j in range(8):
        cast(f"o{j}")
    kTh = [sb([128, SEQ], BF16, f"kTh{i}") for i in range(2)]
    qTh = [sb([128, OWN], BF16, f"qTh{i}") for i in range(2)]
    Vh = [sb([128, 32, 128], BF16, f"Vh{i}") for i in range(2)]
    zaTh = [sb([128, OWN], BF16, f"zaTh{i}") for i in range(2)]
    Et = [[sb([128, 512], BF16, f"Et{m}{i}") for i in range(2)] for m in range(2)]
    onesb = sb([128, 128], BF16, "onesb")
    A("pool", lambda e: e.memset(onesb[:, :], 1.0), writes=["onesb"])
    Osb = [sb([128, 512], F32, f"Osb{m}") for m in range(2)]
    rec = [sb([128, 512], F32, f"rec{m}") for m in range(2)]
    sqb = sb([128, 512], F32, "sqb")
    rsb = sb([128, 512], F32, "rsb")
    yTa = [sb([128, 512], BF16, f"yTa{i}") for i in range(2)]
    Sb = [pbig[0][:, 0:512], pbig[0][:, 512:1024]]
    bO = [pbig[1][:, 0:512], pbig[1][:, 512:1024]]
    bR = [pbig[2][:, 0:512], pbig[2][:, 512:1024]]
    fin_i = [0]
    pending = []
    inflight = []
    ssb = tr_f32[0]

    def fin_math():
        if not pending:
            return
        hh, qq, bb = pending.pop(0)
        for m in range(2):
            A("dve", lambda e: e.reciprocal(out=rec[m][:, :], in_=rec[m][:, :]), reads=[f"rec{m}"], writes=[f"rec{m}"])
            A("dve", lambda e: e.tensor_tensor(out=Osb[m][:, :], in0=Osb[m][:, :], in1=rec[m][:, :], op=ALU.mult),
              reads=[f"Osb{m}", f"rec{m}"], writes=[f"Osb{m}"])
        A("dve", lambda e: e.scalar_tensor_tensor(out=Osb[0][:, :], in0=Osb[1][:, :], scalar=neglam[:, 0:1], in1=Osb[0][:, :],
                                                  op0=ALU.mult, op1=ALU.add),
          reads=["Osb0", "Osb1", "neglam"], writes=["Osb0"])
        A("pool", lambda e: e.tensor_tensor(out=sqb[:, :], in0=Osb[0][:, :], in1=Osb[0][:, :], op=ALU.mult), reads=["Osb0"], writes=["sqb"])
        inflight.append((hh, qq, bb))

    def fin_tail():
        if not inflight:
            return
        hh, qq, bb = inflight.pop(0)
        A("pe", lambda e: e.matmul(ssb[:, :], lhsT=onesf[:, :], rhs=sqb[:, :], start=True, stop=True),
          reads=["sqb", "consts"], writes=["ssb"])
        A("act", lambda e: e.activation(out=rsb[:, :], in_=ssb[:, :], func=AF.Sqrt, scale=1.0 / 128, bias=EPSB),
          reads=["ssb", "cvec"], writes=["rsb"])
        A("dve", lambda e: e.reciprocal(out=rsb[:, :], in_=rsb[:, :]), reads=["rsb"], writes=["rsb"])
        A("dve", lambda e: e.scalar_tensor_tensor(out=Osb[0][:, :], in0=Osb[0][:, :], scalar=subc[:, 0:1], in1=rsb[:, :],
                                                  op0=ALU.mult, op1=ALU.mult),
          reads=["Osb0", "rsb", "subc8"], writes=["Osb0"])
        yb = fin_i[0] % 2
        fin_i[0] += 1
        A("pool", lambda e: e.tensor_tensor(out=yTa[yb][:, :], in0=Osb[0][:, :], in1=zaTh[bb][:, qq * 512:(qq + 1) * 512], op=ALU.mult),
          reads=["Osb0", f"zaTh{bb}"], writes=[f"yTa{yb}"])
        A("pool", lambda e: e.dma_start(out=yT_d[2048 + hh * 128:2048 + (hh + 1) * 128, qq * 512:(qq + 1) * 512], in_=yTa[yb][:, :]),
          reads=[f"yTa{yb}"], dma=f"yTa{yb}")

    def flush_all():
        fin_math()
        fin_tail()

    for h in range(NH):
        b = h % 2
        A("sp", lambda e: e.dma_start(out=kTh[b][:, :], in_=kT_d[h, :, :]), writes=[f"kTh{b}"], dma=f"kTh{b}")
        A("sp", lambda e: e.dma_start(out=qTh[b][:, :], in_=qT_d[h, :, :]), writes=[f"qTh{b}"], dma=f"qTh{b}")
        A("sp", lambda e: e.dma_start(out=Vh[b][:, :, :], in_=v_d[:, h * 128:(h + 1) * 128].rearrange("(k p) e -> p k e", p=128)),
          writes=[f"Vh{b}"], dma=f"Vh{b}")
        A("sp", lambda e: e.dma_start(out=zaTh[b][:, :], in_=zaT_d[h, :, :]), writes=[f"zaTh{b}"], dma=f"zaTh{b}")
        for qc in range(4):
            kb_diag = 16 + 4 * qc
            npair = (kb_diag + 4) // 2

            nkb = kb_diag + 4

            def geom(kb):
                di = kb - kb_diag
                if di < 0:
                    return di, 0
                return di, (di - (di % 2)) * 128

            def qk(kb):
                di, n0 = geom(kb)
                t = kb % 2
                for m in range(2):
                    et = Et[m][t]
                    bank = Sb[m]
                    key = f"bS{m}"
                    A("pe", lambda e: e.matmul(
                        bank[:, n0:512], lhsT=kTh[b][m * 64:(m + 1) * 64, kb * 128:(kb + 1) * 128],
                        rhs=qTh[b][m * 64:(m + 1) * 64, qc * 512 + n0:(qc + 1) * 512], start=True, stop=True),
                      reads=[f"kTh{b}", f"qTh{b}"], writes=[key])
                    if kb < 16:
                        A("act", lambda e: e.activation(out=et[:, n0:512], in_=bank[:, n0:512], func=AF.Exp, bias=pbias[:, 0:1]),
                          reads=[key, "consts"], writes=[f"Et{m}{t}"])
                    else:
                        A("act", lambda e: e.activation(out=et[:, n0:512], in_=bank[:, n0:512], func=AF.Exp),
                          reads=[key], writes=[f"Et{m}{t}"])
                    if di >= 0:
                        A("dve", lambda e: e.tensor_tensor(out=et[:, n0:n0 + 256], in0=et[:, n0:n0 + 256], in1=mask2[:, t, :], op=ALU.mult),
                          reads=[f"Et{m}{t}", "consts"], writes=[f"Et{m}{t}"])

            def pv(kb):
                di, n0 = geom(kb)
                t = kb % 2
                for m in range(2):
                    et = Et[m][t]
                    A("pe", lambda e: e.matmul(bO[m][:, n0:512], lhsT=Vh[b][:, kb, :], rhs=et[:, n0:512],
                                               start=(kb == 0), stop=(kb == nkb - 1)),
                      reads=[f"Et{m}{t}", f"Vh{b}"], writes=[f"bO{m}"])
                    A("pe", lambda e: e.matmul(bR[m][:, n0:512], lhsT=onesb[:, :], rhs=et[:, n0:512],
                                               start=(kb == 0), stop=(kb == nkb - 1)),
                      reads=[f"Et{m}{t}", "onesb"], writes=[f"bR{m}"])

            qk(0)
            for kb in range(nkb):
                if kb + 1 < nkb:
                    qk(kb + 1)
                pv(kb)
                if kb == 1:
                    fin_math()
                if kb == 9:
                    fin_tail()
            A("act", lambda e: e.copy(out=Osb[0][:, :], in_=bO[0][:, :]), reads=["bO0"], writes=["Osb0"])
            A("dve", lambda e: e.tensor_copy(out=Osb[1][:, :], in_=bO[1][:, :]), reads=["bO1"], writes=["Osb1"])
            A("act", lambda e: e.copy(out=rec[0][:, :], in_=bR[0][:, :]), reads=["bR0"], writes=["rec0"])
            A("dve", lambda e: e.tensor_copy(out=rec[1][:, :], in_=bR[1][:, :]), reads=["bR1"], writes=["rec1"])
            pending.append((h, qc, b))
    flush_all()

    if not do_out:
        P.emit()
        return nc
    P.barrier()
    cur[0] = P1
    yT = sb([128, KC, TT], BF16, "yT")
    wpo = [sb([128, KC, 512], BF16, f"wpo{i}") for i in range(2)]
    out_sb = [sb([128, D], F32, f"out_sb{i}") for i in range(4)]
    postw = sb([128, D], F32, "postw")
    xres = sb([128, D], F32, "xres")
    ssq = sb([128, 4, 8], F32, "ssq")
    st4 = sb([128, 8], F32, "st4")
    junk4 = sb([128, 512], BF16, "junk4")
    A("sp", lambda e: e.dma_start(out=postw[:, :], in_=postw_d[:, :]), writes=["postw"], dma="postw")
    po_i = [0]
    for tile in range(4):
        A("sp", lambda e: e.dma_start(out=out_d[tile * TT:(tile + 1) * TT, :], in_=x_d[OWN + tile * TT:OWN + (tile + 1) * TT, :]),
          writes=[f"xcopy{tile}"], dma=f"xcopy{tile}")
    for tile in range(4):
        o0 = tile * TT
        for s in range(4):
            A("sp", lambda e: e.dma_start(out=yT[:, :, s * 128:(s + 1) * 128],
                                          in_=yT_d[:, o0 + s * 128:o0 + (s + 1) * 128].rearrange("(kc p) t -> p kc t", p=128)),
              writes=[f"yT{s}"], dma=f"yT{s}")
        for n in range(8):
            slot = po_i[0] % 2
            po_i[0] += 1
            A("sp", (lambda n=n, slot=slot: lambda e: e.dma_start(out=wpo[slot][:, :, :], in_=wbo[n][:, :, :]))(),
              reads=[f"wb_o{n}"], writes=[f"wpo{slot}"], dma=f"wpo{slot}")
            for s in range(4):
                bank = mm_i[0] % 2
                mm_i[0] += 1
                for kc in range(KC):
                    A("pe", (lambda kc=kc, s=s, slot=slot, bank=bank: lambda e: e.matmul(
                        ps_mm[bank][:, :], lhsT=yT[:, kc, s * 128:(s + 1) * 128], rhs=wpo[slot][:, kc, :],
                        start=(kc == 0), stop=(kc == KC - 1)))(),
                      reads=[f"wpo{slot}", f"yT{s}"], writes=[f"ps_mm{bank}"])
                A("dve", (lambda s=s, n=n, bank=bank: lambda e: e.tensor_copy(out=out_sb[s][:, n * 512:(n + 1) * 512], in_=ps_mm[bank][:, :]))(),
                  reads=[f"ps_mm{bank}"], writes=[f"out_sb{s}"])
                A("act", (lambda s=s, n=n: lambda e: e.activation(out=junk4[:, :], in_=out_sb[s][:, n * 512:(n + 1) * 512], func=AF.Square,
                                                                accum_out=ssq[:, s, n:n + 1]))(),
                  reads=[f"out_sb{s}"], writes=["junk4", f"ssq{s}"])
                if n == 7:
                    r0 = OWN + o0 + s * 128
                    A("dve", (lambda s=s: lambda e: e.tensor_reduce(out=st4[:, 0:1], in_=ssq[:, s, :], axis=AX.X, op=ALU.add))(),
                      reads=[f"ssq{s}"], writes=["st40"])
                    A("act", lambda e: e.activation(out=st4[:, 1:2], in_=st4[:, 0:1], func=AF.Sqrt, scale=1.0 / D, bias=EPSB),
                      reads=["st40", "cvec"], writes=["st41"])
                    A("dve", lambda e: e.reciprocal(out=st4[:, 2:3], in_=st4[:, 1:2]), reads=["st41"], writes=["st42"])
                    A("dve", (lambda s=s: lambda e: e.scalar_tensor_tensor(out=out_sb[s][:, :], in0=out_sb[s][:, :], scalar=st4[:, 2:3],
                                                                         in1=postw[:, :], op0=ALU.mult, op1=ALU.mult))(),
                      reads=[f"out_sb{s}", "st42", "postw"], writes=[f"out_sb{s}"])
                    A("pool", (lambda s=s, o0=o0: lambda e: e.dma_start(out=out_d[o0 + s * 128:o0 + (s + 1) * 128, :], in_=out_sb[s][:, :],
                                                                      accum_op=ALU.add, max_dma_last_dim=8192))(),
                      reads=[f"out_sb{s}", f"xcopy{tile}"], dma=f"out_sb{s}", out=True)

    P.emit()
    return nc


_CACHE = {}
_DEBUG = False


def _host_consts():
    tri = np.triu(np.ones((128, 128), np.float32))
    negm = np.where(tri > 0, 0.0, NEG).astype(np.float32)
    m2 = np.zeros((128, 2, 256), np.float32)
    m2[:, 0, :128] = tri
    m2[:, 0, 128:] = 1.0
    m2[:, 1, 128:] = tri
    return dict(tri_f=tri, negm=negm, tri_b=tri.astype(ml_dtypes.bfloat16), mask2=m2.astype(ml_dtypes.bfloat16),
                ident_b=np.eye(128, dtype=np.float32).astype(ml_dtypes.bfloat16),
                ones_f=np.ones((128, 128), np.float32))


def kernel(x, pre_norm_w, post_norm_w, w_in, conv_w, conv_b, dt_bias, a_log, d_skip, ssm_norm_w,
           lambda_q1, lambda_k1, lambda_q2, lambda_k2, attn_subln_w, w_out):
    f32 = np.float32
    x = np.asarray(x, f32)
    perm = _perm_cols()
    w_in0 = np.asarray(w_in, f32)[0]
    w_in_p = np.ascontiguousarray(w_in0[:, perm])
    w_dt = np.ascontiguousarray(w_in0[:, 7168:7200])
    w_out0 = np.ascontiguousarray(np.asarray(w_out, f32)[0])
    rep = lambda v: np.ascontiguousarray(np.broadcast_to(np.asarray(v, f32).reshape(1, -1), (128, np.asarray(v).size)))
    ch = []
    for g in range(NG):
        ch += list(range(g * 512, (g + 1) * 512))
        ch += list(range(2048 + g * 128, 2048 + (g + 1) * 128))
        ch += list(range(2560 + g * 128, 2560 + (g + 1) * 128))
    ch = np.array(ch)
    cw = np.asarray(conv_w, f32)[0][:, ch]
    cw = np.ascontiguousarray(cw.reshape(4, 24, 128).transpose(2, 1, 0))
    cb = np.ascontiguousarray(np.asarray(conv_b, f32)[0][ch].reshape(24, 128).T)
    headvec = np.ascontiguousarray(np.broadcast_to(
        np.stack([np.asarray(dt_bias, f32)[0], np.asarray(a_log, f32)[0], np.asarray(d_skip, f32)[0]])[None], (128, 3, 32)))
    lam = np.ascontiguousarray(np.broadcast_to(
        np.stack([np.asarray(v, f32)[0] for v in (lambda_q1, lambda_k1, lambda_q2, lambda_k2)])[None], (128, 4, 64)))
    common = dict(
        w_in=w_in_p, w_dt=w_dt, w_out=w_out0,
        pre_w=np.ascontiguousarray(np.asarray(pre_norm_w, f32)[0].reshape(KC, 128).T),
        post_w=rep(np.asarray(post_norm_w)[0]), conv_w=cw, conv_b=cb, headvec=headvec,
        ssm_norm_w=rep(np.asarray(ssm_norm_w)[0]), lam=lam, subln_w=rep(np.asarray(attn_subln_w)[0]),
        subw_col=np.ascontiguousarray(np.asarray(attn_subln_w, f32)[0].reshape(128, 1)),
        **_host_consts())
    in_maps = []
    for c in range(8):
        b, j = c // 2, c % 2
        if j == 0:
            xl = np.concatenate([np.zeros((OWN, D), f32), x[b, :OWN]], axis=0)
        else:
            xl = x[b]
        m = dict(common)
        m["x"] = np.ascontiguousarray(xl)
        m["pmask"] = np.full((128, 1), float(j), f32)
        m["pbias"] = np.full((128, 1), 0.0 if j else NEG, f32)
        in_maps.append(m)
    if _DEBUG:
        return in_maps
    if "nc" not in _CACHE:
        _CACHE["nc"] = build_program()
    res = run_bass_kernel_spmd(_CACHE["nc"], in_maps, core_ids=list(range(8)))
    out = np.empty((NBATCH, SEQ, D), f32)
    for c in range(8):
        b, j = c // 2, c % 2
        out[b, j * OWN:(j + 1) * OWN] = res.results[c]["out"]
    return out
```
